# Optimizing a Trainium2 kernel written in Bass

```python
import math
import jax, jax.numpy as jnp
from jax import lax
import numpy as np

D_MODEL = 1024
BATCH = 8
SEQ = 2048
DEPTH = 4

GRID_W = 64
CTX_LEN = 256
N_BRANCH = 4
CHUNK = 128
A_GROUPS = 4
A_GDIM = 64
A_WIDTH = A_GROUPS * A_GDIM
B_WIDTH = 256
CONV_W = 3
C_HEADS = 4
C_HDIM = 64
C_VDIM = 2 * C_HDIM
C_WIDTH = C_HEADS * C_VDIM
D_GROUPS = 4
D_GDIM = 64
D_WIDTH = D_GROUPS * D_GDIM
A_COLS = 2 * A_WIDTH
B_COLS = 3 * B_WIDTH
QK_COLS = C_HEADS * 2 * C_HDIM
V_COLS = C_HEADS * C_VDIM
IN_COLS = A_COLS + B_COLS + 2 * QK_COLS + V_COLS + D_WIDTH
SPLIT_AT = (A_COLS, A_COLS + B_COLS, A_COLS + B_COLS + QK_COLS, A_COLS + B_COLS + 2 * QK_COLS, A_COLS + B_COLS + 2 * QK_COLS + V_COLS)
N_EXPERTS = 16
CAP_FACTOR = 2
EXPERT_HIDDEN = 1024
Q_BLOCK = 128
ROPE_BASE = 10000.0
DN_ALPHA = (2 * DEPTH) ** 0.25
DN_BETA = (8 * DEPTH) ** -0.25
LN_EPS = 1e-5
RMS_EPS = 1e-5

kernel_name = 'hybrid_diffusion_gated_mixers_ec_moe'


def layer_norm(x, g=None, b=None):
    xf = x.astype(jnp.float32)
    mu = jnp.mean(xf, axis=-1, keepdims=True)
    var = jnp.mean(jnp.square(xf - mu), axis=-1, keepdims=True)
    y = (xf - mu) * lax.rsqrt(var + LN_EPS)
    if g is not None:
        y = y * g.astype(jnp.float32) + b.astype(jnp.float32)
    return y.astype(x.dtype)


def adaln_params(cond, w, b):
    m = jax.nn.silu(cond) @ w + b
    return jnp.split(m[..., None, :], 6, axis=-1)


def modulate(x, shift, scale):
    return layer_norm(x) * (1 + scale) + shift


def axial_rope_tables(n):
    rows = n // GRID_W
    t = jnp.arange(rows * GRID_W)
    row = (t // GRID_W).astype(jnp.float32)
    col = (t % GRID_W).astype(jnp.float32)
    half = C_HDIM // 2
    inv = ROPE_BASE ** (-jnp.arange(0, half, 2, dtype=jnp.float32) / half)
    ang = jnp.concatenate([row[:, None] * inv, col[:, None] * inv], axis=-1)
    return jnp.cos(ang), jnp.sin(ang)


def apply_rope(t, cos, sin):
    bsz, n, h, i, d = t.shape
    tp = t.astype(jnp.float32).reshape(bsz, n, h, i, d // 2, 2)
    cs = cos[None, :, None, None, :]
    sn = sin[None, :, None, None, :]
    x1, x2 = tp[..., 0], tp[..., 1]
    out = jnp.stack([x1 * cs - x2 * sn, x1 * sn + x2 * cs], axis=-1)
    return out.reshape(t.shape).astype(t.dtype)


def split_proj(z):
    za, zb, q, k, v, zd = jnp.split(z, SPLIT_AT, axis=-1)
    bsz, n = z.shape[:2]
    q = q.reshape(bsz, n, C_HEADS, 2, C_HDIM)
    k = k.reshape(bsz, n, C_HEADS, 2, C_HDIM)
    v = v.reshape(bsz, n, C_HEADS, C_VDIM)
    return za, zb, q, k, v, zd


def chunk_sgu(za, ln_g, ln_b, w_sp, b_sp):
    z = jax.nn.gelu(za)
    u, v = jnp.split(z, 2, axis=-1)
    v = layer_norm(v, ln_g, ln_b)
    bsz, n, _ = v.shape
    v = v.reshape(bsz, n // CHUNK, CHUNK, A_GROUPS, A_GDIM)
    mix = jnp.einsum('gpq,bcqgd->bcpgd', w_sp, v) + b_sp.T[:, :, None]
    return u * mix.reshape(bsz, n, A_WIDTH)


def short_conv_gate(zb, conv_w):
    gb, gc, xv = jnp.split(zb, 3, axis=-1)
    y = gc * xv
    yp = jnp.pad(y, ((0, 0), (1, 1), (0, 0)))
    conv = yp[:, :-2] * conv_w[0] + yp[:, 1:-1] * conv_w[1] + yp[:, 2:] * conv_w[2]
    return gb * conv


def fourier_mix(zd):
    bsz, n, _ = zd.shape
    zg = zd.astype(jnp.float32).reshape(bsz, n, D_GROUPS, D_GDIM)
    f = jnp.fft.fft2(zg, axes=(1, 3), norm='ortho')
    return jnp.real(f).astype(zd.dtype).reshape(bsz, n, D_WIDTH)


def diff_attention(q, k, v, lam):
    s = jnp.einsum('bqhid,bkhid->bhiqk', q.astype(jnp.float32), k.astype(jnp.float32)) * (C_HDIM ** -0.5)
    p = jax.nn.softmax(s, axis=-1)
    a = p[:, :, 0] - lam * p[:, :, 1]
    return jnp.einsum('bhqk,bkhe->bqhe', a, v.astype(jnp.float32)).astype(v.dtype)


def blocked_diff_attention(q, k, v, lam):
    bsz, n = q.shape[:2]
    nb = n // Q_BLOCK
    qb = jnp.moveaxis(q.reshape(bsz, nb, Q_BLOCK, C_HEADS, 2, C_HDIM), 1, 0)
    out = lax.map(lambda qq: diff_attention(qq, k, v, lam), qb)
    return jnp.moveaxis(out, 0, 1).reshape(bsz, n, C_HEADS, C_VDIM)


def diff_post(o, subln_g, lam_init):
    bsz, n = o.shape[:2]
    of = o.astype(jnp.float32)
    of = of * lax.rsqrt(jnp.mean(of * of, axis=-1, keepdims=True) + RMS_EPS) * subln_g.astype(jnp.float32) * (1.0 - lam_init)
    return of.astype(o.dtype).reshape(bsz, n, C_WIDTH)


def merged_mixers(h, za, zb, yc, zd, sgu_ln_g, sgu_ln_b, w_sp, b_sp, conv_w, w_gate, b_gate, w_pa, w_pb, w_pc, w_pd, w_o):
    ya = chunk_sgu(za, sgu_ln_g, sgu_ln_b, w_sp, b_sp)
    yb = short_conv_gate(zb, conv_w)
    yd = fourier_mix(zd)
    g = jax.nn.sigmoid(h @ w_gate + b_gate)
    ga, gb, gc, gd = jnp.split(g, N_BRANCH, axis=-1)
    m = ga * (ya @ w_pa) + gb * (yb @ w_pb) + gc * (yc @ w_pc) + gd * (yd @ w_pd)
    return m @ w_o


def expert_choice_moe(h, w_router, w_g, w_u, w_d):
    bsz, n, _ = h.shape
    cap = (CAP_FACTOR * n) // N_EXPERTS
    aff = jax.nn.softmax((h @ w_router).astype(jnp.float32), axis=-1)
    gate, idx = lax.top_k(jnp.swapaxes(aff, 1, 2), cap)
    bidx = jnp.arange(bsz)[:, None, None]
    xe = h[bidx, idx]
    hid = jax.nn.silu(jnp.einsum('becd,edf->becf', xe, w_g)) * jnp.einsum('becd,edf->becf', xe, w_u)
    ye = jnp.einsum('becf,efd->becd', hid, w_d) * gate[..., None].astype(h.dtype)
    return jnp.zeros_like(h).at[bidx, idx].add(ye)


def setup_inputs(seed: int = 0) -> dict:
    key = jax.random.key(seed)
    ks = iter(jax.random.split(key, 40))
    L, D, E, F = DEPTH, D_MODEL, N_EXPERTS, EXPERT_HIDDEN

    def nrm(shape, scale):
        return jax.random.normal(next(ks), shape, jnp.float32) * scale

    return {
        'x': nrm((BATCH, SEQ, D), 1.0),
        'c': nrm((BATCH, D), 1.0),
        'ctx': nrm((BATCH, CTX_LEN, D), 1.0),
        'c_ctx': nrm((D,), 1.0),
        'w_ada': nrm((L, D, 6 * D), 0.5 * D ** -0.5),
        'b_ada': nrm((L, 6 * D), 0.01),
        'w_in': nrm((L, D, IN_COLS), D ** -0.5),
        'w_gate': nrm((L, D, N_BRANCH * D), D ** -0.5),
        'b_gate': nrm((L, N_BRANCH * D), 0.01),
        'sgu_ln_g': 1.0 + nrm((L, A_WIDTH), 0.01),
        'sgu_ln_b': nrm((L, A_WIDTH), 0.01),
        'w_sp': nrm((L, A_GROUPS, CHUNK, CHUNK), CHUNK ** -0.5),
        'b_sp': 1.0 + nrm((L, A_GROUPS, CHUNK), 0.01),
        'conv_w': nrm((L, CONV_W, B_WIDTH), CONV_W ** -0.5),
        'lam_q1': nrm((L, C_HDIM), 0.1),
        'lam_k1': nrm((L, C_HDIM), 0.1),
        'lam_q2': nrm((L, C_HDIM), 0.1),
        'lam_k2': nrm((L, C_HDIM), 0.1),
        'subln_g': 1.0 + nrm((L, C_VDIM), 0.01),
        'w_pa': nrm((L, A_WIDTH, D), A_WIDTH ** -0.5),
        'w_pb': nrm((L, B_WIDTH, D), B_WIDTH ** -0.5),
        'w_pc': nrm((L, C_WIDTH, D), C_WIDTH ** -0.5),
        'w_pd': nrm((L, D_WIDTH, D), D_WIDTH ** -0.5),
        'w_o': nrm((L, D, D), DN_BETA * D ** -0.5),
        'ln1_g': 1.0 + nrm((L, D), 0.01),
        'ln1_b': nrm((L, D), 0.01),
        'w_router': nrm((L, D, E), D ** -0.5),
        'w_exp_gate': nrm((L, E, D, F), D ** -0.5),
        'w_exp_up': nrm((L, E, D, F), D ** -0.5),
        'w_exp_down': nrm((L, E, F, D), DN_BETA * F ** -0.5),
        'ln2_g': 1.0 + nrm((L, D), 0.01),
        'ln2_b': nrm((L, D), 0.01),
    }


def reference(x, c, ctx, c_ctx, w_ada, b_ada, w_in, w_gate, b_gate, sgu_ln_g, sgu_ln_b, w_sp, b_sp, conv_w, lam_q1, lam_k1, lam_q2, lam_k2, subln_g, w_pa, w_pb, w_pc, w_pd, w_o, ln1_g, ln1_b, w_router, w_exp_gate, w_exp_up, w_exp_down, ln2_g, ln2_b):
    xl, xc = x, ctx
    n = x.shape[1]
    cos, sin = axial_rope_tables(n)
    for l in range(DEPTH):
        last = l == DEPTH - 1
        lam_init = 0.8 - 0.6 * math.exp(-0.3 * l)
        lam = (jnp.exp(jnp.sum(lam_q1[l].astype(jnp.float32) * lam_k1[l].astype(jnp.float32)))
               - jnp.exp(jnp.sum(lam_q2[l].astype(jnp.float32) * lam_k2[l].astype(jnp.float32))) + lam_init)
        ada_l = adaln_params(c, w_ada[l], b_ada[l])
        ada_c = adaln_params(c_ctx, w_ada[l], b_ada[l])
        mix_w = (sgu_ln_g[l], sgu_ln_b[l], w_sp[l], b_sp[l], conv_w[l], w_gate[l], b_gate[l],
                 w_pa[l], w_pb[l], w_pc[l], w_pd[l], w_o[l])
        moe_w = (w_router[l], w_exp_gate[l], w_exp_up[l], w_exp_down[l])

        hl = modulate(xl, ada_l[0], ada_l[1])
        hc = modulate(xc, ada_c[0], ada_c[1])
        za, zb, q, k, v, zd = split_proj(hl @ w_in[l])
        zca, zcb, qc, kc, vc, zcd = split_proj(hc @ w_in[l])
        q = apply_rope(q, cos, sin)
        k = apply_rope(k, cos, sin)
        k_all = jnp.concatenate([kc, k], axis=1)
        v_all = jnp.concatenate([vc, v], axis=1)
        yc_l = diff_post(blocked_diff_attention(q, k_all, v_all, lam), subln_g[l], lam_init)
        mix_l = merged_mixers(hl, za, zb, yc_l, zd, *mix_w)
        xl = layer_norm(DN_ALPHA * xl + ada_l[2] * mix_l, ln1_g[l], ln1_b[l])

        if not last:
            yc_c = diff_post(diff_attention(qc, kc, vc, lam), subln_g[l], lam_init)
            mix_c = merged_mixers(hc, zca, zcb, yc_c, zcd, *mix_w)
            xc = layer_norm(DN_ALPHA * xc + ada_c[2] * mix_c, ln1_g[l], ln1_b[l])
            hc2 = modulate(xc, ada_c[3], ada_c[4])
            xc = layer_norm(DN_ALPHA * xc + ada_c[5] * expert_choice_moe(hc2, *moe_w), ln2_g[l], ln2_b[l])

        hl2 = modulate(xl, ada_l[3], ada_l[4])
        xl = layer_norm(DN_ALPHA * xl + ada_l[5] * expert_choice_moe(hl2, *moe_w), ln2_g[l], ln2_b[l])
    return xl
```

```python
import math
from contextlib import ExitStack

import numpy as np
import ml_dtypes
import concourse.bass as bass
import concourse.mybir as mybir
from concourse.bass_utils import run_bass_kernel_spmd

F32 = mybir.dt.float32
BF16 = mybir.dt.bfloat16
I32 = mybir.dt.int32
U32 = mybir.dt.uint32
AF = mybir.ActivationFunctionType
ALU = mybir.AluOpType
AX = mybir.AxisListType

D = 1024
SEQ = 2048
CTX = 256
NTOK = SEQ + CTX
NT = NTOK // 128
NTL = SEQ // 128
DEPTH = 4
NE = 16
CAPL = 256
CAPC = 32
DN_ALPHA = (2 * DEPTH) ** 0.25
LN_EPS = 1e-5
RMS_EPS = 1e-5
TB = [(i * 512, 512) for i in range(4)] + [(2048, 256)]
_ESZ = {}


def _esize(dt):
    s = _ESZ.get(dt)
    if s is None:
        s = 2 if dt == BF16 else 4
        _ESZ[dt] = s
    return s


class Ev:
    __slots__ = ("key", "val", "clock")

    def __init__(self, key, val, clock):
        self.key = key
        self.val = val
        self.clock = clock


def _box(ap):
    t = ap.tensor
    name = t.name
    aps = ap.ap
    off = ap.offset
    es = _esize(ap.dtype)
    tn = type(t).__name__
    if "PSum" in tn:
        return (name, 0, 128, 0, 2048, True)
    if "DRam" in tn:
        ext = 0
        for st, cnt in aps:
            ext += abs(st) * (cnt - 1)
        return (name, 0, 1, off * es, (off + ext + 1) * es)
    pstep = aps[0][0]
    if pstep == 0:
        p0 = 0
        f0 = off
    else:
        p0 = off // pstep
        f0 = off - p0 * pstep
    pcnt = aps[0][1]
    ext = 0
    for st, cnt in aps[1:]:
        ext += abs(st) * (cnt - 1)
    return (name, p0, p0 + pcnt, f0 * es, (f0 + ext + 1) * es)


class KB:
    def __init__(self, nc, dma_ring=8):
        self.nc = nc
        self.eng = {"pe": nc.tensor, "act": nc.scalar, "dve": nc.vector, "pool": nc.gpsimd, "sp": nc.sync}
        self.sems = {}
        self.cnt = {}
        for e in self.eng:
            self.sems[e] = nc.alloc_semaphore(name=f"s_{e}")
            self.cnt[e] = 0
        self.known = {e: {} for e in self.eng}
        self.ring = {}
        self.ring_i = {}
        for q in ("sp", "pool", "act"):
            self.ring[q] = []
            for i in range(dma_ring):
                key = f"d_{q}{i}"
                self.sems[key] = nc.alloc_semaphore(name=key)
                self.cnt[key] = 0
                self.ring[q].append(key)
            self.ring_i[q] = 0
        self.acc = {}
        self.alias = {}
        self.n_inst = 0
        self.n_wait = 0

    def _name(self, b):
        return self.alias.get(b[0], b[0])

    def _deps(self, reads, writes):
        deps = []
        for ap in reads:
            b = _box(ap)
            d = self.acc.get(self._name(b))
            if not d:
                continue
            ps_ = len(b) > 5
            for (ob, w, key), ev in d.items():
                if (w or ps_) and ob[1] < b[2] and b[1] < ob[2] and ob[3] < b[4] and b[3] < ob[4]:
                    deps.append(ev)
        for ap in writes:
            b = _box(ap)
            d = self.acc.get(self._name(b))
            if not d:
                continue
            for (ob, w, key), ev in d.items():
                if ob[1] < b[2] and b[1] < ob[2] and ob[3] < b[4] and b[3] < ob[4]:
                    deps.append(ev)
        return deps

    def _record(self, reads, writes, ev):
        for ap in reads:
            b = _box(ap)
            d = self.acc.setdefault(self._name(b), {})
            d[(b, len(b) > 5, ev.key)] = ev
        for ap in writes:
            b = _box(ap)
            d = self.acc.setdefault(self._name(b), {})
            if len(b) <= 5:
                dead = [k for k in d
                        if k[0][1] >= b[1] and k[0][2] <= b[2] and k[0][3] >= b[3] and k[0][4] <= b[4]]
                for k in dead:
                    del d[k]
            d[(b, True, ev.key)] = ev

    def _wait_for(self, e, deps, skip_same=False):
        kn = self.known[e]
        need = {}
        for ev in deps:
            if skip_same and ev.key == e:
                continue
            if kn.get(ev.key, 0) >= ev.val:
                continue
            if need.get(ev.key, (0, None))[0] < ev.val:
                need[ev.key] = (ev.val, ev)
        items = sorted(need.items(), key=lambda kv: -len(kv[1][1].clock))
        for key, (val, ev) in items:
            if kn.get(key, 0) >= val:
                continue
            self.eng[e].wait_ge(self.sems[key], val)
            self.n_wait += 1
            for k2, v2 in ev.clock.items():
                if kn.get(k2, 0) < v2:
                    kn[k2] = v2
            if kn.get(key, 0) < val:
                kn[key] = val

    def op(self, e, fn, reads=(), writes=()):
        deps = self._deps(reads, writes)
        self._wait_for(e, deps, skip_same=(e == "pe"))
        ins = fn(self.eng[e])
        self.cnt[e] += 1
        v = self.cnt[e]
        ins.then_inc(self.sems[e], 1)
        clock = dict(self.known[e])
        clock[e] = v
        ev = Ev(e, v, clock)
        self._record(reads, writes, ev)
        self.n_inst += 1
        return ev

    def dma_raw(self, q, fn, reads, writes, extra_deps=(), record_writes=None):
        deps = self._deps(reads, writes)
        deps.extend(extra_deps)
        i = self.ring_i[q]
        self.ring_i[q] = i + 1
        key = self.ring[q][i % len(self.ring[q])]
        prev = self.cnt[key]
        if prev and self.known[q].get(key, 0) < prev:
            self.eng[q].wait_ge(self.sems[key], prev)
            self.known[q][key] = prev
            self.n_wait += 1
        self._wait_for(q, deps)
        ins = fn(self.eng[q])
        ins.then_inc(self.sems[key], 16)
        self.cnt[key] = prev + 16
        clock = dict(self.known[q])
        clock[key] = prev + 16
        ev = Ev(key, prev + 16, clock)
        self._record(reads, writes if record_writes is None else record_writes, ev)
        self.n_inst += 1
        return ev

    def dma(self, q, out, in_, **kw):
        return self.dma_raw(q, lambda e: e.dma_start(out=out, in_=in_, **kw), [in_], [out])

    def finish(self):
        e = "sp"
        for key, c in self.cnt.items():
            if c and self.known[e].get(key, 0) < c:
                self.eng[e].wait_ge(self.sems[key], c)
                self.known[e][key] = c


def _w_in_perm():
    perm = []
    perm += list(range(0, 512))
    for f in range(2):
        perm += list(range(768 + f * 128, 768 + (f + 1) * 128))
        perm += list(range(1024 + f * 128, 1024 + (f + 1) * 128))
        perm += list(range(512 + f * 128, 512 + (f + 1) * 128))
        perm += list(range(2816 + f * 128, 2816 + (f + 1) * 128))
    for base0 in (1280, 1792):
        de, sw = [], []
        for blk in range(8):
            b = base0 + blk * 64
            ev = [b + 2 * j for j in range(32)]
            od = [b + 2 * j + 1 for j in range(32)]
            de += ev + od
            sw += od + ev
        perm += de + sw
    perm += list(range(2304, 2816))
    return np.array(perm, dtype=np.int64)


def _consts():
    c = {}
    c["ident"] = np.eye(128, dtype=np.float32)
    t = np.arange(SEQ)
    row = (t // 64).astype(np.float32)
    col = (t % 64).astype(np.float32)
    inv = (10000.0 ** (-np.arange(0, 32, 2, dtype=np.float32) / 32)).astype(np.float32)
    ang = np.concatenate([row[:, None] * inv, col[:, None] * inv], axis=-1).astype(np.float32)
    cs = np.cos(ang).astype(np.float32)
    sn = np.sin(ang).astype(np.float32)
    cosT = np.ones((128, NTOK), np.float32)
    sinT = np.zeros((128, NTOK), np.float32)
    for p in range(128):
        r = p % 64
        j = r % 32
        cosT[p, :SEQ] = cs[:, j]
        sinT[p, :SEQ] = -sn[:, j] if r < 32 else sn[:, j]
    c["ropecos"] = cosT
    c["ropesin"] = sinT
    def dft(n):
        k = np.arange(n, dtype=np.float64)
        a = 2 * np.pi * np.outer(k, k) / n
        return np.cos(a), np.sin(a)
    C, S = dft(SEQ)
    c["dftC"] = (C / math.sqrt(SEQ)).astype(ml_dtypes.bfloat16)
    c["dftS"] = (-S / math.sqrt(SEQ)).astype(ml_dtypes.bfloat16)
    C, S = dft(CTX)
    c["dftCc"] = (C / math.sqrt(CTX)).astype(ml_dtypes.bfloat16)
    c["dftSc"] = (-S / math.sqrt(CTX)).astype(ml_dtypes.bfloat16)
    C, S = dft(64)
    c64 = np.zeros((128, 256), np.float32)
    for g in range(2):
        c64[g * 64:(g + 1) * 64, g * 64:(g + 1) * 64] = C / 8.0
        c64[g * 64:(g + 1) * 64, 128 + g * 64:128 + (g + 1) * 64] = S / 8.0
    c["c64"] = c64.astype(ml_dtypes.bfloat16)
    c["zeros"] = np.zeros((128, D), np.float32)
    return c


_CONST_CACHE = {}


def _get_consts():
    if not _CONST_CACHE:
        _CONST_CACHE.update(_consts())
    return _CONST_CACHE


def build(layers=None, dbg=(), stop_after=None):
    nc = bass.Bass("TRN2", target_bir_lowering=False)
    if layers is None:
        layers = list(range(DEPTH))
    L = len(layers)
    nl = layers[-1] + 1

    def din(name, shape, dt=F32):
        return nc.dram_tensor(name, list(shape), dt, kind="ExternalInput").ap()

    xin = din("xin", [NTOK, D])
    cc = din("cc", [2, D])
    w_ada = din("w_ada", [L, D, 6 * D])
    b_ada = din("b_ada", [L, 6 * D])
    w_in = din("w_in", [L, D, 4096])
    w_gate = din("w_gate", [L, 8, D, 512])
    b_gate = din("b_gate", [L, 4 * D])
    sgu_g = din("sgu_ln_g", [L, 256])
    sgu_b = din("sgu_ln_b", [L, 256])
    w_spT = din("w_spT", [L, 4, 128, 128])
    b_sp = din("b_sp", [L, 4, 128])
    conv_w = din("conv_w", [L, 3, 256])
    lam_in = din("lam_in", [L, 4, 64])
    subln_g = din("subln_g", [L, 128])
    w_p = din("w_p", [L, 8, 1280, 128])
    w_o = din("w_o", [L, D, D])
    ln_gb = din("ln_gb", [L, 4, D])
    w_router = din("w_router", [L, D, NE])
    w_eg = din("w_exp_gate", [L, NE, D, D])
    w_eu = din("w_exp_up", [L, NE, D, D])
    w_ed = din("w_exp_down", [L, NE, D, D])
    ident_d = din("ident", [128, 128])
    ropecos = din("ropecos", [128, NTOK])
    ropesin = din("ropesin", [128, NTOK])
    dftC = din("dftC", [SEQ, SEQ], BF16)
    dftS = din("dftS", [SEQ, SEQ], BF16)
    dftCc = din("dftCc", [CTX, CTX], BF16)
    dftSc = din("dftSc", [CTX, CTX], BF16)
    c64_d = din("c64", [128, 256], BF16)
    zeros_d = din("zeros", [128, D])

    y = nc.dram_tensor("y", [SEQ, D], F32, kind="ExternalOutput").ap()
    xbuf = nc.dram_tensor("xbuf", [NTOK, D], F32, kind="Internal").ap()
    xn2 = nc.dram_tensor("xn2", [NTOK + 128, D], F32, kind="Internal").ap()
    macc = nc.dram_tensor("macc", [NTOK + 128, D], F32, kind="Internal").ap()
    dbg_out = {}

    def dout(name, shape, dt=F32):
        t = nc.dram_tensor(name, list(shape), dt, kind="ExternalOutput").ap()
        dbg_out[name] = t
        return t

    es = ExitStack()
    with es:
        kb = KB(nc)

        def sb(name, shape, dt=F32):
            return es.enter_context(nc.sbuf_tensor(name, list(shape), dt))

        banks = [es.enter_context(nc.psum_tensor(f"ps{i}", [128, 512], F32)) for i in range(8)]
        bank_i = [0]
        held = set()

        def ps(hold=False):
            while True:
                i = bank_i[0] % 8
                bank_i[0] += 1
                if i not in held:
                    break
            if hold:
                held.add(i)
            return banks[i]

        def release(b):
            held.discard(banks.index(b))

        def mm(out, lhsT, rhs, start=True, stop=True, **kw):
            return kb.op("pe", lambda e: e.matmul(out, lhsT, rhs, start=start, stop=stop, **kw),
                         reads=[lhsT, rhs], writes=[out])

        def tr(out, in_):
            k_ = in_.shape[0]
            return kb.op("pe", lambda e: e.transpose(out, in_, ident[0:k_, 0:k_]), reads=[in_, ident[0:k_, 0:k_]],
                         writes=[out])

        def act(out, in_, func, bias=None, scale=None, accum_out=None, eng="act"):
            reads = [in_]
            kw = {}
            if bias is not None:
                kw["bias"] = bias
                if not isinstance(bias, float):
                    reads.append(bias)
            if scale is not None:
                kw["scale"] = scale
                if not isinstance(scale, float):
                    reads.append(scale)
            writes = [out]
            if accum_out is not None:
                kw["accum_out"] = accum_out
                writes.append(accum_out)
            return kb.op("act", lambda e: e.activation(out, in_, func, **kw), reads=reads, writes=writes)

        def tt(eng, out, in0, in1, op):
            if eng == "pool":
                eng = "dve"
            return kb.op(eng, lambda e: e.tensor_tensor(out, in0, in1, op), reads=[in0, in1], writes=[out])

        def tt_pool(out, in0, in1, op):
            return kb.op("pool", lambda e: e.tensor_tensor(out, in0, in1, op), reads=[in0, in1], writes=[out])

        def ts(eng, out, in0, s1, s2, op0, op1=None, accum_out=None):
            if eng == "pool":
                eng = "dve"
            reads = [in0]
            for s in (s1, s2):
                if s is not None and not isinstance(s, (float, int)):
                    reads.append(s)
            writes = [out]
            kw = {}
            if op1 is not None:
                kw["op1"] = op1
            if accum_out is not None:
                kw["accum_out"] = accum_out
                writes.append(accum_out)
            return kb.op(eng, lambda e: e.tensor_scalar(out, in0, s1, s2, op0, **kw), reads=reads, writes=writes)

        def stt(eng, out, in0, scalar, in1, op0, op1):
            eng = "dve"
            reads = [in0, in1]
            if not isinstance(scalar, (float, int)):
                reads.append(scalar)
            return kb.op(eng, lambda e: e.scalar_tensor_tensor(out, in0, scalar, in1, op0, op1),
                         reads=reads, writes=[out])

        def cp(eng, out, in_):
            if eng == "pool":
                eng = "dve"
            if eng == "act":
                return kb.op("act", lambda e: e.copy(out, in_), reads=[in_], writes=[out])
            return kb.op(eng, lambda e: e.tensor_copy(out, in_), reads=[in_], writes=[out])

        def memset(eng, ap, val):
            if eng == "pool":
                eng = "dve"
            return kb.op(eng, lambda e: e.memset(ap, val), reads=[], writes=[ap])

        ident = sb("ident_sb", [128, 128])
        c64 = sb("c64_sb", [128, 256], BF16)
        hT = sb("hT", [128, 8, NTOK], BF16)
        ARENA = 44032
        arena = sb("arena", [128, ARENA], BF16)
        ybr = arena[:, 0:23040].rearrange("p (a b) -> p a b", a=10)
        MT0 = 23040
        mT = arena[:, MT0:MT0 + 18432].rearrange("p (a b) -> p a b", a=8)
        kT = arena[:, MT0:MT0 + 9216].rearrange("p (a b) -> p a b", a=4)
        vaug = arena[:, MT0 + 9216:MT0 + 9216 + 9288].rearrange("p (t h e) -> p t h e", t=NT, h=4)
        QT0 = MT0 + 18560
        qblk = arena[:, QT0:QT0 + 2048].rearrange("p (a b) -> p a b", a=4)
        ycv = arena[:, MT0:MT0 + 9216].bitcast(F32)
        AB = arena[:, MT0 + 9216:MT0 + 18432].rearrange("p (t c) -> p t c", t=NT)
        xnT_r = arena[:, 0:2048].bitcast(F32).rearrange("p (k n) -> p k n", k=8)
        affT = arena[0:NE, 2048:2048 + 4608].bitcast(F32)
        affW = arena[0:NE, 2048 + 4608:2048 + 9216].bitcast(F32)
        topv = arena[0:NE, 0:576].bitcast(F32)
        topi = arena[0:NE, 576:1152].bitcast(U32)
        topif = arena[0:NE, 1152:1728].bitcast(F32)
        xeT = arena[:, 2048:2048 + 2304].rearrange("p (k n) -> p k n", k=8)
        hidT = arena[:, 2048 + 2304:2048 + 4608].rearrange("p (k n) -> p k n", k=8)
        EW0 = 11264
        ewslot = [arena[:, EW0 + i * 8192:EW0 + (i + 1) * 8192].rearrange("p (k n) -> p k n", k=8) for i in range(4)]
        hT_flat = hT[:].rearrange("p a b -> p (a b)")
        ewslot += [hT_flat[:, i * 8192:(i + 1) * 8192].rearrange("p (k n) -> p k n", k=8) for i in range(2)]
        ew_i = [0]

        def next_ew():
            w = ewslot[ew_i[0] % 6]
            ew_i[0] += 1
            return w

        NW = 3
        wbuf = [sb(f"wbuf{i}", [128, 8, 512], BF16) for i in range(NW)]
        wb_i = [0]

        def next_wbuf():
            w = wbuf[wb_i[0] % NW]
            wb_i[0] += 1
            return w

        xt = [sb(f"xt{i}", [128, D]) for i in range(3)]
        xt_i = [0]

        def next_xt():
            t = xt[xt_i[0] % 3]
            xt_i[0] += 1
            return t

        tmpa = [sb(f"tmpa{i}", [128, 512]) for i in range(4)]
        tmpa_i = [0]

        def next_tmp():
            t = tmpa[tmpa_i[0] % 4]
            tmpa_i[0] += 1
            return t

        ebuf = [sb(f"ebuf{i}", [128, 512], BF16) for i in range(4)]
        ebuf_i = [0]

        def next_e():
            t = ebuf[ebuf_i[0] % 4]
            ebuf_i[0] += 1
            return t

        stat = sb("stat", [128, 64])
        stat_i = [0]

        def next_stat(n):
            if stat_i[0] + n > 64:
                stat_i[0] = 0
            s_ = stat[:, stat_i[0]:stat_i[0] + n]
            stat_i[0] += n
            return s_

        sc_col = sb("sc_col", [128, 8, 2], BF16)
        c_col = sb("c_col", [128, 8, 2])
        modT = sb("modT", [128, 4, 8, 2])
        badaT = sb("badaT", [128, 48])
        gate_bc = sb("gate_bc", [128, 2, D])
        lngb = sb("lngb", [128, 2, D])
        sgugb = sb("sgugb", [128, 2, 256])
        wsp = sb("wsp", [128, 4, 128], BF16)
        bsp_row = sb("bsp_row", [1, 512])
        ones_row = sb("ones_row", [1, 128])
        convT = sb("convT", [128, 2, 3])
        conv_rows = sb("conv_rows", [6, 128])
        bgT = sb("bgT", [128, 32])
        rows48 = sb("rows48", [48, 128])
        lam_t = sb("lam_t", [128, 4, 64])
        lam_s = sb("lam_s", [128, 4])
        subg = sb("subg", [128, 128])
        wpb = [sb("wpb0", [128, 10, 128], BF16), sb("wpb1", [128, 10, 128], BF16)]
        wr_sb = sb("wr_sb", [128, 8, NE])
        wr_mod = sb("wr_mod", [128, 8, 2, NE])
        rb_bc = sb("rb_bc", [128, 2, NE])
        idxT = sb("idxT", [128, 3, NE], U32)
        eps_t = sb("eps_t", [128, 1])
        gatT = sb("gatT", [128, 3, NE])

        kb.dma("sp", ident[:], ident_d)
        kb.dma("sp", c64[:], c64_d)
        memset("pool", ones_row[:], 1.0)
        memset("pool", eps_t[:], LN_EPS)
        for (p0_, p1_) in ((32, 64), (64, 128)):
            kb.op("pool", lambda e: e.iota(idxT[p0_:p1_, 2, :], [[0, NE]], base=NTOK + p0_, channel_multiplier=1),
                  reads=[], writes=[idxT[p0_:p1_, 2, :]])
        for t in range(NT):
            x_t = next_xt()
            kb.dma("sp", x_t[:], xin[t * 128:(t + 1) * 128, :])
            kb.dma("sp", xbuf[t * 128:(t + 1) * 128, :], x_t[:])
        with nc.allow_non_contiguous_dma(reason="tiny column loads"):
            for s in range(2):
                kb.dma("sp", c_col[:, :, s], cc[s, :].rearrange("(a p) -> p a", p=128))
        act(sc_col[:], c_col[:], AF.Silu)

        def load_rows_T(dst_cols, src_rows_ap, nrows, stage):
            kb.dma("sp", stage[0:nrows, :], src_rows_ap)
            p = ps()
            tr(p[:, 0:nrows], stage[0:nrows, :])
            cp("dve", dst_cols, p[:, 0:nrows])

        def bcast_row(dst, src_row_ap):
            kb.dma("sp", dst, src_row_ap.partition_broadcast(128))

        p0_done = set()
        p1_done = set()

        def layer(lreal, part="all"):
            last = (lreal == DEPTH - 1)
            lam_init = 0.8 - 0.6 * math.exp(-0.3 * lreal)
            l = layers.index(lreal)
            ntl = NTL if last else NT
            tbl = TB[:4] if last else TB

            def load_gate(gi):
                for half in range(2):
                    j = (4 if gi == 0 else 10) + half
                    wb = next_wbuf()
                    kb.dma("pool", wb[:], w_ada[l, :, j * 512:(j + 1) * 512].rearrange("(k p) n -> p k n", p=128))
                    brow = next_tmp()
                    bcast_row(brow[:], b_ada[l:l + 1, j * 512:(j + 1) * 512])
                    for s in range(2):
                        p = ps()
                        for kc in range(8):
                            mm(p[:], sc_col[:, kc, s:s + 1].to_broadcast([128, 128]), wb[:, kc, :],
                               start=(kc == 0), stop=(kc == 7))
                        tt("dve", gate_bc[:, s, half * 512:(half + 1) * 512], p[:], brow[:], ALU.add)

            if lreal not in p0_done:
                load_rows_T(badaT[:], b_ada[l].rearrange("(a p) -> a p", p=128), 48, rows48)
                for j in (0, 1, 2, 3, 6, 7, 8, 9):
                    wb = next_wbuf()
                    kb.dma("pool", wb[:], w_ada[l, :, j * 512:(j + 1) * 512].rearrange("(k p) n -> p k n", p=128))
                    vec = j // 2
                    mi = {0: 0, 1: 1, 3: 2, 4: 3}[vec]
                    p = ps()
                    import os
                    if os.environ.get("KDBG") == "nomm":
                        continue
                    for sub in range(4):
                        for kc in range(8):
                            mm(p[:, sub * 2:sub * 2 + 2], wb[:, kc, sub * 128:(sub + 1) * 128], sc_col[:, kc, :],
                               start=(kc == 0), stop=(kc == 7))
                    if os.environ.get("KDBG") == "noevac":
                        continue
                    for sub in range(4):
                        dc = (j % 2) * 4 + sub
                        col = j * 4 + sub
                        if mi in (1, 3):
                            ts("dve", modT[:, mi, dc, :], p[:, sub * 2:sub * 2 + 2], badaT[:, col:col + 1], 1.0,
                               ALU.add, ALU.add)
                        else:
                            ts("dve", modT[:, mi, dc, :], p[:, sub * 2:sub * 2 + 2], badaT[:, col:col + 1], None,
                               ALU.add)
                bcast_row(sgugb[:, 0, :], sgu_g[l:l + 1, :])
                bcast_row(sgugb[:, 1, :], sgu_b[l:l + 1, :])
                kb.dma("pool", wsp[:], w_spT[l].rearrange("g q p -> q g p"))
                kb.dma("sp", bsp_row[:], b_sp[l:l + 1].rearrange("o g p -> o (g p)"))
                kb.dma("sp", conv_rows[:], conv_w[l].rearrange("k (f p) -> (k f) p", p=128))
                p = ps()
                tr(p[:, 0:6], conv_rows[:])
                cp("dve", convT[:].rearrange("p f k -> p k f"), p[:, 0:6].rearrange("p (k f) -> p k f", k=3))
                load_rows_T(bgT[:], b_gate[l].rearrange("(a p) -> a p", p=128), 32, rows48)
                bcast_row(subg[:], subln_g[l:l + 1, :])
                ts("pool", subg[:], subg[:], 1.0 - lam_init, None, ALU.mult)
                kb.dma("sp", lam_t[:].rearrange("p a b -> p (a b)"),
                       lam_in[l:l + 1].rearrange("o a b -> o (a b)").partition_broadcast(128))
                lt = next_tmp()
                for i in range(2):
                    tt("dve", lt[:, i * 64:(i + 1) * 64], lam_t[:, 2 * i, :], lam_t[:, 2 * i + 1, :], ALU.mult)
                    kb.op("dve", lambda e: e.reduce_sum(lam_s[:, i:i + 1], lt[:, i * 64:(i + 1) * 64], AX.X),
                          reads=[lt[:, i * 64:(i + 1) * 64]], writes=[lam_s[:, i:i + 1]])
                act(lam_s[:, 0:2], lam_s[:, 0:2], AF.Exp)
                tt("dve", lam_s[:, 2:3], lam_s[:, 0:1], lam_s[:, 1:2], ALU.subtract)
                ts("dve", lam_s[:, 3:4], lam_s[:, 2:3], lam_init, -1.0, ALU.add, ALU.mult)

                p0_done.add(lreal)
            if part == "p0":
                return
            if stop_after == "P0":
                return

            def ln_stats(x_ap, width):
                nchunk = (width + 511) // 512
                st = next_stat(16)
                bst = next_tmp()
                for c_ in range(nchunk):
                    w_ = min(512, width - c_ * 512)
                    kb.op("dve", lambda e: e.bn_stats(bst[:, c_ * 6:(c_ + 1) * 6], x_ap[:, c_ * 512:c_ * 512 + w_]),
                          reads=[x_ap[:, c_ * 512:c_ * 512 + w_]], writes=[bst[:, c_ * 6:(c_ + 1) * 6]])
                kb.op("dve", lambda e: e.bn_aggr(st[:, 0:2], bst[:, 0:nchunk * 6]),
                      reads=[bst[:, 0:nchunk * 6]], writes=[st[:, 0:2]])
                act(st[:, 2:3], st[:, 1:2], AF.Ln, bias=eps_t[:, 0:1])
                act(st[:, 2:3], st[:, 2:3], AF.Exp, scale=-0.5)
                stt("dve", st[:, 3:4], st[:, 0:1], -1.0, st[:, 2:3], ALU.mult, ALU.mult)
                ln_stats.nbias = st[:, 3:4]
                return st[:, 0:1], st[:, 2:3]

            def normalize_to_hT(x_t, t, mi_shift, dst=None, also_xn2=False, router=False):
                xn = norm_stage(x_t, t, also_xn2)
                tr_stage(xn, t, mi_shift, dst, router)
                return xn

            def norm_stage(x_t, t, also_xn2=False):
                mean, rstd = ln_stats(x_t, D)
                xn = x_t
                act(xn[:], x_t[:], AF.Identity, bias=ln_stats.nbias, scale=rstd)
                if also_xn2:
                    kb.dma("pool", xn2[t * 128:(t + 1) * 128, :], xn[:])
                return xn

            def tr_stage(xn, t, mi_shift, dst=None, router=False):
                s = 0 if t < NTL else 1
                for half in range(2):
                    p = ps()
                    for k4 in range(4):
                        kc = half * 4 + k4
                        tr(p[:, k4 * 128:(k4 + 1) * 128], xn[:, kc * 128:(kc + 1) * 128])
                    for k4 in range(4):
                        kc = half * 4 + k4
                        if dst is not None:
                            if half == 0:
                                act(dst[:, kc, t * 128:(t + 1) * 128], p[:, k4 * 128:(k4 + 1) * 128], AF.Identity,
                                    bias=modT[:, mi_shift, kc, s:s + 1], scale=modT[:, mi_shift + 1, kc, s:s + 1])
                            else:
                                ts("dve", dst[:, kc, t * 128:(t + 1) * 128], p[:, k4 * 128:(k4 + 1) * 128],
                                   modT[:, mi_shift + 1, kc, s:s + 1], modT[:, mi_shift, kc, s:s + 1],
                                   ALU.mult, ALU.add)
                        if router:
                            xr = xnT_r[:, kc, :]
                            if half == 1:
                                cp("dve", xr, p[:, k4 * 128:(k4 + 1) * 128])
                            else:
                                cp("act", xr, p[:, k4 * 128:(k4 + 1) * 128])

            def p1_load_norm(t):
                x_t = next_xt()
                kb.dma("sp", x_t[:], xbuf[t * 128:(t + 1) * 128, :])
                return norm_stage(x_t, t)

            if lreal not in p1_done:
                xn_cur = p1_load_norm(0)
                for t in range(NT):
                    xn_nxt = p1_load_norm(t + 1) if t + 1 < NT else None
                    tr_stage(xn_cur, t, 0, dst=hT)
                    xn_cur = xn_nxt
            if "hT" in dbg and lreal == nl - 1:
                kb.dma("sp", dbg_out["d_hT"].rearrange("p (a b) -> p a b", a=8), hT[:])
            if stop_after == "P1":
                return

            def load_win(ci):
                wb = next_wbuf()
                kb.dma("pool", wb[:], w_in[l, :, ci * 512:(ci + 1) * 512].rearrange("(k p) n -> p k n", p=128))
                return wb

            def fm_group(p_ap, wb, col0, tok0, ntok, ncol=128):
                for kc in range(8):
                    mm(p_ap, wb[:, kc, col0:col0 + ncol], hT[:, kc, tok0:tok0 + ntok], start=(kc == 0), stop=(kc == 7))

            def tm_group(p_ap, wb, col0, ncol, t):
                for kc in range(8):
                    mm(p_ap, hT[:, kc, t * 128:(t + 1) * 128], wb[:, kc, col0:col0 + ncol],
                       start=(kc == 0), stop=(kc == 7))

            wb = load_win(0)
            for fc in range(2):
                for (t0, n) in TB:
                    p = ps()
                    fm_group(p[:, 0:n], wb, fc * 128, t0, n)
                    act(ybr[:, fc, t0:t0 + n], p[:, 0:n], AF.Gelu_apprx_tanh)
            for t in range(NT):
                p = ps()
                tm_group(p[:, 0:256], wb, 256, 256, t)
                vt = next_tmp()
                act(vt[:, 0:256], p[:, 0:256], AF.Gelu_apprx_tanh)
                mean, rstd = ln_stats(vt[:, 0:256], 256)
                ts("dve", vt[:, 0:256], vt[:, 0:256], mean, rstd, ALU.subtract, ALU.mult)
                tt("pool", vt[:, 0:256], vt[:, 0:256], sgugb[:, 0, :], ALU.mult)
                vn = next_e()
                tt("pool", vn[:, 0:256], vt[:, 0:256], sgugb[:, 1, :], ALU.add)
                p2 = ps()
                for gp in range(2):
                    for g2 in range(2):
                        g = gp * 2 + g2
                        o = p2[g2 * 64:(g2 + 1) * 64, gp * 128:(gp + 1) * 128]
                        mm(o, vn[:, g * 64:(g + 1) * 64], wsp[:, g, :], start=True, stop=False)
                        mm(o, ones_row[0:1, 0:64], bsp_row[0:1, g * 128:(g + 1) * 128], start=False, stop=True)
                for gp in range(2):
                    tt("dve", ybr[:, gp, t * 128:(t + 1) * 128], ybr[:, gp, t * 128:(t + 1) * 128],
                       p2[:, gp * 128:(gp + 1) * 128], ALU.mult)
            if "ya" in dbg and lreal == nl - 1:
                kb.dma("sp", dbg_out["d_ya"].rearrange("p (a b) -> p a b", a=2), ybr[:, 0:2, :])
            if stop_after == "ya":
                return

            ybuf = ycv[:, 0:NTOK]
            cvbuf = ycv[:, NTOK:2 * NTOK]
            for f in range(2):
                wb = load_win(1 + f)
                for (t0, n) in TB:
                    pa = ps()
                    fm_group(pa[:, 0:n], wb, 0, t0, n)
                    pb = ps()
                    fm_group(pb[:, 0:n], wb, 128, t0, n)
                    tg = next_tmp()
                    cp("act", tg[:, 0:n], pa[:, 0:n])
                    tt("dve", ybuf[:, t0:t0 + n], tg[:, 0:n], pb[:, 0:n], ALU.mult)
                for (t0, n) in TB:
                    pz = ps()
                    fm_group(pz[:, 0:n], wb, 384, t0, n)
                    cp("act", ybr[:, 4 + f, t0:t0 + n], pz[:, 0:n])
                ts("pool", cvbuf, ybuf, convT[:, f, 1:2], None, ALU.mult)
                for (s0, n) in ((0, SEQ), (SEQ, CTX)):
                    stt("pool", cvbuf[:, s0 + 1:s0 + n], ybuf[:, s0:s0 + n - 1], convT[:, f, 0:1],
                        cvbuf[:, s0 + 1:s0 + n], ALU.mult, ALU.add)
                    stt("pool", cvbuf[:, s0:s0 + n - 1], ybuf[:, s0 + 1:s0 + n], convT[:, f, 2:3],
                        cvbuf[:, s0:s0 + n - 1], ALU.mult, ALU.add)
                for (t0, n) in TB:
                    pg = ps()
                    fm_group(pg[:, 0:n], wb, 256, t0, n)
                    tt("dve", ybr[:, 2 + f, t0:t0 + n], pg[:, 0:n], cvbuf[:, t0:t0 + n], ALU.mult)
            if "yb" in dbg and lreal == nl - 1:
                kb.dma("sp", dbg_out["d_yb"].rearrange("p (a b) -> p a b", a=2), ybr[:, 2:4, :])

            for t in range(NT):
                p = ps()
                for f in range(2):
                    for cs_ in range(2):
                        mm(p[:, cs_ * 256 + f * 128: cs_ * 256 + (f + 1) * 128], ybr[:, 4 + f, t * 128:(t + 1) * 128],
                           c64[:, cs_ * 128:(cs_ + 1) * 128], start=True, stop=True)
                if t % 2 == 0:
                    cp("act", AB[:, t, :], p[:])
                else:
                    cp("dve", AB[:, t, :], p[:])
            pacc = {}
            for nb in range(4):
                for f in range(2):
                    pacc[(nb, f)] = ps(hold=True)
            for nt_ in range(NTL):
                wc = next_wbuf()
                wc_v = wc[:].rearrange("p k n -> p (k n)")
                kb.dma("sp", wc_v[:, 0:2048], dftC[nt_ * 128:(nt_ + 1) * 128, :])
                kb.dma("sp", wc_v[:, 2048:4096], dftS[nt_ * 128:(nt_ + 1) * 128, :])
                for nb in range(4):
                    for f in range(2):
                        mm(pacc[(nb, f)][:], AB[:, nt_, f * 128:(f + 1) * 128], wc_v[:, nb * 512:(nb + 1) * 512],
                           start=(nt_ == 0), stop=False)
                        mm(pacc[(nb, f)][:], AB[:, nt_, 256 + f * 128:256 + (f + 1) * 128],
                           wc_v[:, 2048 + nb * 512:2048 + (nb + 1) * 512], start=False, stop=(nt_ == NTL - 1))
            for nb in range(4):
                for f in range(2):
                    if f == 0:
                        cp("act", ybr[:, 4 + f, nb * 512:(nb + 1) * 512], pacc[(nb, f)][:])
                    else:
                        cp("dve", ybr[:, 4 + f, nb * 512:(nb + 1) * 512], pacc[(nb, f)][:])
            for v_ in pacc.values():
                release(v_)
            wc = next_wbuf()
            wc_v = wc[:].rearrange("p k n -> p (k n)")
            for c2 in range(2):
                kb.dma("sp", wc_v[:, c2 * 512:c2 * 512 + 256], dftCc[c2 * 128:(c2 + 1) * 128, :])
                kb.dma("sp", wc_v[:, c2 * 512 + 256:c2 * 512 + 512], dftSc[c2 * 128:(c2 + 1) * 128, :])
            for f in range(2):
                p = ps()
                for c2 in range(2):
                    mm(p[:, 0:256], AB[:, NTL + c2, f * 128:(f + 1) * 128], wc_v[:, c2 * 512:c2 * 512 + 256],
                       start=(c2 == 0), stop=False)
                    mm(p[:, 0:256], AB[:, NTL + c2, 256 + f * 128:256 + (f + 1) * 128],
                       wc_v[:, c2 * 512 + 256:c2 * 512 + 512], start=False, stop=(c2 == 1))
                cp("act", ybr[:, 4 + f, SEQ:NTOK], p[:, 0:256])
            if "yd" in dbg and lreal == nl - 1:
                kb.dma("sp", dbg_out["d_yd"].rearrange("p (a b) -> p a b", a=2), ybr[:, 4:6, :])
            if stop_after == "yd":
                return

            def rope_proj(ci, dst, tbs, wa=None, ws=None, dst_off=None):
                if wa is None:
                    wa = load_win(ci)
                    ws = load_win(ci + 1)
                for fc in range(4):
                    for (t0, n) in tbs:
                        d0 = t0 if dst_off is None else dst_off
                        pa = ps()
                        fm_group(pa[:, 0:n], wa, fc * 128, t0, n)
                        pb = ps()
                        fm_group(pb[:, 0:n], ws, fc * 128, t0, n)
                        ct = next_tmp()
                        st_ = next_tmp()
                        kb.dma("sp", ct[:, 0:n], ropecos[:, t0:t0 + n])
                        kb.dma("sp", st_[:, 0:n], ropesin[:, t0:t0 + n])
                        tt("dve", ct[:, 0:n], pa[:, 0:n], ct[:, 0:n], ALU.mult)
                        tt("dve", st_[:, 0:n], pb[:, 0:n], st_[:, 0:n], ALU.mult)
                        tt("pool", dst[:, fc, d0:d0 + n], ct[:, 0:n], st_[:, 0:n], ALU.add)

            rope_proj(5, kT, TB)
            wv = load_win(7)
            memset("pool", vaug[:, :, :, 128:129], 1.0)
            for t in range(NT):
                p = ps()
                tm_group(p[:], wv, 0, 512, t)
                if t % 2 == 0:
                    cp("act", vaug[:, t, :, 0:128], p[:].rearrange("p (h e) -> p h e", h=4))
                else:
                    cp("dve", vaug[:, t, :, 0:128], p[:].rearrange("p (h e) -> p h e", h=4))
            if "qk" in dbg and lreal == nl - 1:
                kb.dma("sp", dbg_out["d_k"].rearrange("p (a b) -> p a b", a=4), kT[:])

            wq_a = load_win(3)
            wq_s = load_win(4)

            def attention(q0, nq, ktiles):
                nqs = nq // 128
                rope_proj(3, qblk, [(q0, nq)], wa=wq_a, ws=wq_s, dst_off=0)
                if "qk" in dbg and lreal == nl - 1:
                    for fc_ in range(4):
                        kb.dma("sp", dbg_out["d_q"][:, fc_ * NTOK + q0:fc_ * NTOK + q0 + nq], qblk[:, fc_, 0:nq])
                for h in range(4):
                    nacc = 2 * nqs
                    nb_ = (nacc + 2) // 3
                    accb = [ps(hold=True) for _ in range(nb_)]
                    for b_ in accb:
                        memset("dve", b_[:], 0.0)

                    def acc_ap(i, qs):
                        a = i * nqs + qs
                        return accb[a // 3][:, (a % 3) * 160:(a % 3) * 160 + 129]

                    steps = [(i, kt) for i in range(2) for kt in ktiles]
                    pend = []
                    nxt = 0
                    LOOK = 3
                    for s_ in range(len(steps)):
                        while nxt < len(steps) and len(pend) <= LOOK:
                            i_, kt_ = steps[nxt]
                            blk = h * 2 + i_
                            ch, pb_ = blk // 2, (blk % 2) * 64
                            sp_ = ps()
                            mm(sp_[:, 0:nq], kT[pb_:pb_ + 64, ch, kt_ * 128:(kt_ + 1) * 128],
                               qblk[pb_:pb_ + 64, ch, 0:nq], start=True, stop=True)
                            e_ = next_e()
                            act(e_[:, 0:nq], sp_[:, 0:nq], AF.Exp, scale=0.125)
                            pend.append(e_)
                            nxt += 1
                        e_ = pend.pop(0)
                        i, kt = steps[s_]
                        for qs in range(nqs):
                            mm(acc_ap(i, qs), e_[:, qs * 128:(qs + 1) * 128], vaug[:, kt, h, :],
                               start=False, stop=False, skip_group_check=True)
                    for qs in range(nqs):
                        a1 = acc_ap(0, qs)
                        a2 = acc_ap(1, qs)
                        st = next_stat(8)
                        cp("dve", st[:, 0:1], a1[:, 128:129])
                        cp("dve", st[:, 1:2], a2[:, 128:129])
                        kb.op("dve", lambda e: e.reciprocal(st[:, 2:4], st[:, 0:2]), reads=[st[:, 0:2]],
                              writes=[st[:, 2:4]])
                        tt("dve", st[:, 3:4], st[:, 3:4], lam_s[:, 3:4], ALU.mult)
                        o = next_tmp()
                        ts("dve", o[:, 0:128], a1[:, 0:128], st[:, 2:3], None, ALU.mult)
                        stt("dve", o[:, 0:128], a2[:, 0:128], st[:, 3:4], o[:, 0:128], ALU.mult, ALU.add)
                        act(o[:, 128:256], o[:, 0:128], AF.Square)
                        kb.op("dve", lambda e: e.reduce_sum(st[:, 4:5], o[:, 128:256], AX.X),
                              reads=[o[:, 128:256]], writes=[st[:, 4:5]])
                        act(st[:, 5:6], st[:, 4:5], AF.Ln, bias=eps_t[:, 0:1], scale=1.0 / 128)
                        act(st[:, 5:6], st[:, 5:6], AF.Exp, scale=-0.5)
                        stt("dve", o[:, 256:384], o[:, 0:128], st[:, 5:6], subg[:], ALU.mult, ALU.mult)
                        pt = ps()
                        tr(pt[:, 0:128], o[:, 256:384])
                        cp("act", ybr[:, 6 + h, q0 + qs * 128:q0 + (qs + 1) * 128], pt[:, 0:128])
                    for b_ in accb:
                        release(b_)

            for qb in range(4):
                attention(qb * 512, 512, list(range(NT)))
            if not last:
                attention(SEQ, CTX, [NTL, NTL + 1])
            if "yc" in dbg and lreal == nl - 1:
                kb.dma("sp", dbg_out["d_yc"].rearrange("p (a b) -> p a b", a=4), ybr[:, 6:10, :])
            if stop_after == "yc":
                return

            def load_p4(dc_):
                wg_ = next_wbuf()
                kb.dma("pool", wg_[:], w_gate[l, dc_].rearrange("(k p) n -> p k n", p=128))
                wp__ = wpb[dc_ % 2]
                kb.dma("pool", wp__[:], w_p[l, dc_].rearrange("(k p) n -> p k n", p=128))
                return wg_, wp__

            nxt_w = load_p4(0)
            for dc in range(8):
                wg, wp_ = nxt_w
                if dc + 1 < 8:
                    nxt_w = load_p4(dc + 1)
                chunks = [(0, 2), (2, 4), (6, 10), (4, 6)]
                for (t0, n) in tbl:
                    macc_t = next_xt()
                    for br in range(4):
                        pg = ps()
                        fm_group(pg[:, 0:n], wg, br * 128, t0, n)
                        pp = ps()
                        c0, c1 = chunks[br]
                        for c_ in range(c0, c1):
                            mm(pp[:, 0:n], wp_[:, c_, :], ybr[:, c_, t0:t0 + n], start=(c_ == c0), stop=(c_ == c1 - 1))
                        sg = next_tmp()
                        act(sg[:, 0:n], pg[:, 0:n], AF.Sigmoid, bias=bgT[:, br * 8 + dc:br * 8 + dc + 1])
                        if br == 0:
                            tt("dve", macc_t[:, 0:n], sg[:, 0:n], pp[:, 0:n], ALU.mult)
                        else:
                            tt("dve", sg[:, 0:n], sg[:, 0:n], pp[:, 0:n], ALU.mult)
                            if br < 3:
                                tt("pool", macc_t[:, 0:n], macc_t[:, 0:n], sg[:, 0:n], ALU.add)
                            else:
                                tt("pool", mT[:, dc, t0:t0 + n], macc_t[:, 0:n], sg[:, 0:n], ALU.add)
            if "mT" in dbg and lreal == nl - 1:
                kb.dma("sp", dbg_out["d_mT"].rearrange("p (a b) -> p a b", a=8), mT[:])

            kb.dma("sp", wr_sb[:], w_router[l].rearrange("(k p) n -> p k n", p=128))
            for kc in range(8):
                for s in range(2):
                    ts("pool", wr_mod[:, kc, s, :], wr_sb[:, kc, :], modT[:, 3, kc, s:s + 1], None, ALU.mult)
            for s in range(2):
                p = ps()
                for kc in range(8):
                    mm(p[:, 0:NE], modT[:, 2, kc, s:s + 1].to_broadcast([128, 128]), wr_sb[:, kc, :], start=(kc == 0), stop=(kc == 7))
                cp("dve", rb_bc[:, s, :], p[:, 0:NE])

            slot_order = [0, 4, 5, 1, 2, 3]

            def load_w(ee, which):
                w_ = ewslot[slot_order[(ee % 2) * 3 + which]]
                wsrc = (w_eg, w_eu, w_ed)[which]
                kb.dma("pool", w_[:], wsrc[l, ee].rearrange("(k p) n -> p k n", p=128))
                return w_

            W = {}
            for wh in range(3):
                W[(0, wh)] = load_w(0, wh)
            load_gate(0)
            wo_a = next_wbuf()
            wo_b = next_wbuf()
            kb.dma("pool", wo_a[:], w_o[l, :, 0:512].rearrange("(k p) n -> p k n", p=128))
            kb.dma("pool", wo_b[:], w_o[l, :, 512:1024].rearrange("(k p) n -> p k n", p=128))
            bcast_row(lngb[:, 0, :], ln_gb[l, 0:1, :])
            bcast_row(lngb[:, 1, :], ln_gb[l, 1:2, :])

            def residual_ln(x_old, sub_aps, gi, s, out_t):
                for half in range(2):
                    hs = slice(half * 512, (half + 1) * 512)
                    tt("dve", out_t[:, hs], sub_aps[half], gate_bc[:, s, hs], ALU.mult)
                stt("pool", out_t[:], x_old[:], DN_ALPHA, out_t[:], ALU.mult, ALU.add)
                mean, rstd = ln_stats(out_t, D)
                act(out_t[:], out_t[:], AF.Identity, bias=ln_stats.nbias, scale=rstd)
                tt("dve", out_t[:], out_t[:], lngb[:, 0, :], ALU.mult)
                tt("dve", out_t[:], out_t[:], lngb[:, 1, :], ALU.add)

            def wo_stage_a(t):
                s = 0 if t < NTL else 1
                pa = ps()
                pb = ps()
                for kc in range(8):
                    mm(pa[:], mT[:, kc, t * 128:(t + 1) * 128], wo_a[:, kc, :], start=(kc == 0), stop=(kc == 7))
                for kc in range(8):
                    mm(pb[:], mT[:, kc, t * 128:(t + 1) * 128], wo_b[:, kc, :], start=(kc == 0), stop=(kc == 7))
                x_old = next_xt()
                kb.dma("sp", x_old[:], xbuf[t * 128:(t + 1) * 128, :])
                x_new = next_xt()
                residual_ln(x_old, (pa[:], pb[:]), 0, s, x_new)
                kb.dma("pool", xbuf[t * 128:(t + 1) * 128, :], x_new[:])
                return x_new

            def wo_stage_b(t, x_new):
                s = 0 if t < NTL else 1
                normalize_to_hT(x_new, t, 2, dst=None, also_xn2=True, router=True)
                pl = ps()
                for kc in range(8):
                    mm(pl[:, 0:NE], xnT_r[:, kc, :], wr_mod[:, kc, s, :], start=(kc == 0), stop=(kc == 7))
                lg = next_stat(16)
                tt("dve", lg, pl[:, 0:NE], rb_bc[:, s, :], ALU.add)
                st = next_stat(4)
                kb.op("dve", lambda e: e.reduce_max(st[:, 0:1], lg, AX.X), reads=[lg], writes=[st[:, 0:1]])
                ts("dve", st[:, 1:2], st[:, 0:1], -1.0, None, ALU.mult)
                act(lg, lg, AF.Exp, bias=st[:, 1:2])
                kb.op("dve", lambda e: e.reduce_sum(st[:, 2:3], lg, AX.X), reads=[lg], writes=[st[:, 2:3]])
                kb.op("dve", lambda e: e.reciprocal(st[:, 3:4], st[:, 2:3]), reads=[st[:, 2:3]], writes=[st[:, 3:4]])
                ts("dve", lg, lg, st[:, 3:4], None, ALU.mult)
                pt = ps()
                tr(pt[0:NE, 0:128], lg)
                cp("dve", affT[:, t * 128:(t + 1) * 128], pt[0:NE, 0:128])

            xnew_cur = wo_stage_a(0)
            for t in range(ntl):
                xnew_nxt = wo_stage_a(t + 1) if t + 1 < ntl else None
                wo_stage_b(t, xnew_cur)
                xnew_cur = xnew_nxt
            if "xmid" in dbg and lreal == nl - 1:
                for t in range(ntl):
                    x_t = next_xt()
                    kb.dma("sp", x_t[:], xbuf[t * 128:(t + 1) * 128, :])
                    kb.dma("sp", dbg_out["d_xmid"][t * 128:(t + 1) * 128, :], x_t[:])
                kb.dma("sp", dbg_out["d_aff"], affT[:])
            if stop_after == "mid":
                return

            def topk(src0, n, cap, dst0):
                cp("pool", affW[:, src0:src0 + n], affT[:, src0:src0 + n])
                for r in range(cap // 8):
                    mv = topv[:, dst0 + r * 8:dst0 + (r + 1) * 8]
                    kb.op("dve", lambda e: e.max(mv, affW[:, src0:src0 + n]), reads=[affW[:, src0:src0 + n]],
                          writes=[mv])
                    mi_ = topi[:, dst0 + r * 8:dst0 + (r + 1) * 8]
                    kb.op("dve", lambda e: e.max_index(mi_, mv, affW[:, src0:src0 + n]),
                          reads=[mv, affW[:, src0:src0 + n]], writes=[mi_])
                    if r < cap // 8 - 1:
                        kb.op("dve", lambda e: e.match_replace(affW[:, src0:src0 + n], mv, affW[:, src0:src0 + n], -1.0),
                              reads=[mv, affW[:, src0:src0 + n]], writes=[affW[:, src0:src0 + n]])

            topk(0, SEQ, CAPL, 0)
            if not last:
                topk(SEQ, CTX, CAPC, CAPL)
            ncap = CAPL if last else CAPL + CAPC
            cp("dve", topif[:, 0:ncap], topi[:, 0:ncap])
            if not last:
                ts("dve", topif[:, CAPL:ncap], topif[:, CAPL:ncap], float(SEQ), None, ALU.add)
            njt = 2 if last else 3
            for j in range(njt):
                w_ = 128 if j < 2 else CAPC
                p = ps()
                tr(p[0:w_, 0:NE], topif[:, j * 128:j * 128 + w_])
                cp("dve", idxT[0:w_, j, :], p[0:w_, 0:NE])
                p2 = ps()
                tr(p2[0:w_, 0:NE], topv[:, j * 128:j * 128 + w_])
                cp("dve", gatT[0:w_, j, :], p2[0:w_, 0:NE])
            if "topk" in dbg and lreal == nl - 1:
                kb.dma("sp", dbg_out["d_topv"], topv[:])
                kb.dma("sp", dbg_out["d_topi"], topif[:])
            for t in range(ntl):
                kb.dma("sp", macc[t * 128:(t + 1) * 128, :], zeros_d)

            def gather_rows(dst, idx_col):
                kb.dma_raw("pool", lambda e: e.indirect_dma_start(
                    out=dst, out_offset=None, in_=xn2[:, :],
                    in_offset=bass.IndirectOffsetOnAxis(ap=idx_col, axis=0)),
                    reads=[xn2[:, :], idx_col], writes=[dst])

            def transpose_mod(src_t, dst3, col0, ncols, s):
                for half in range(2):
                    p = ps()
                    for k4 in range(4):
                        kc = half * 4 + k4
                        tr(p[:, k4 * 128:k4 * 128 + ncols], src_t[0:ncols, kc * 128:(kc + 1) * 128])
                    for k4 in range(4):
                        kc = half * 4 + k4
                        if half == 0:
                            act(dst3[:, kc, col0:col0 + ncols], p[:, k4 * 128:k4 * 128 + ncols], AF.Identity,
                                bias=modT[:, 2, kc, s:s + 1], scale=modT[:, 3, kc, s:s + 1])
                        else:
                            ts("dve", dst3[:, kc, col0:col0 + ncols], p[:, k4 * 128:k4 * 128 + ncols],
                               modT[:, 3, kc, s:s + 1], modT[:, 2, kc, s:s + 1], ALU.mult, ALU.add)

            gtiles = [wbuf[i][:].rearrange("p k n -> p (k n)").bitcast(F32)[:, h_ * D:(h_ + 1) * D]
                      for i in range(NW) for h_ in range(2)]
            gt_i = [0]

            def next_gt():
                g_ = gtiles[gt_i[0] % len(gtiles)]
                gt_i[0] += 1
                return g_

            xeT2 = arena[:, 6656:6656 + 2304].rearrange("p (k n) -> p k n", k=8)
            xeTs = [xeT, xeT2]
            nsl = CAPL if last else CAPL + CAPC

            def do_gather(ee):
                gts = []
                for j in range(njt):
                    g_ = next_gt()
                    gather_rows(g_, idxT[:, j, ee:ee + 1])
                    gts.append(g_)
                return gts

            def do_transposes(ee, gts):
                for j in range(njt):
                    w_ = 128 if j < 2 else CAPC
                    transpose_mod(gts[j], xeTs[ee % 2], j * 128, w_, 0 if j < 2 else 1)

            for wh in range(3):
                W[(1, wh)] = load_w(1, wh)
            G = {0: do_gather(0)}
            if NE > 1:
                G[1] = do_gather(1)
            do_transposes(0, G[0])
            prev_sc = []
            for e_ in range(NE):
                xe_ = xeTs[e_ % 2]
                wk_g, wk_u, wk_d = W[(e_, 0)], W[(e_, 1)], W[(e_, 2)]
                if e_ + 2 < NE:
                    G[e_ + 2] = do_gather(e_ + 2)
                for fc in range(8):
                    pg = ps()
                    pu = ps()
                    for kc in range(8):
                        mm(pg[:, 0:nsl], wk_g[:, kc, fc * 128:(fc + 1) * 128], xe_[:, kc, 0:nsl],
                           start=(kc == 0), stop=(kc == 7))
                    for kc in range(8):
                        mm(pu[:, 0:nsl], wk_u[:, kc, fc * 128:(fc + 1) * 128], xe_[:, kc, 0:nsl],
                           start=(kc == 0), stop=(kc == 7))
                    sg = next_tmp()
                    act(sg[:, 0:nsl], pg[:, 0:nsl], AF.Silu)
                    tt("dve", hidT[:, fc, 0:nsl], sg[:, 0:nsl], pu[:, 0:nsl], ALU.mult)
                if e_ + 2 < NE:
                    W[(e_ + 2, 0)] = load_w(e_ + 2, 0)
                    W[(e_ + 2, 1)] = load_w(e_ + 2, 1)
                if e_ + 1 < NE:
                    do_transposes(e_ + 1, G[e_ + 1])
                yes = []
                for j in range(njt):
                    w_ = 128 if j < 2 else CAPC
                    ye = next_xt()
                    for half in range(2):
                        p = ps()
                        for fc in range(8):
                            mm(p[0:w_, :], hidT[:, fc, j * 128:j * 128 + w_], wk_d[:, fc, half * 512:(half + 1) * 512],
                               start=(fc == 0), stop=(fc == 7))
                        if half == 0:
                            act(ye[0:w_, 0:512], p[0:w_, :], AF.Copy, scale=gatT[0:w_, j, e_:e_ + 1])
                        else:
                            ts("dve", ye[0:w_, 512:1024], p[0:w_, :], gatT[0:w_, j, e_:e_ + 1], None, ALU.mult)
                    yes.append(ye)
                cur_sc = []
                for j in range(njt):
                    ye = yes[j]
                    ev_ = kb.dma_raw("pool", lambda e: e.indirect_dma_start(
                        out=macc[:, :], out_offset=bass.IndirectOffsetOnAxis(ap=idxT[:, j, e_:e_ + 1], axis=0),
                        in_=ye[:], in_offset=None, compute_op=ALU.add),
                        reads=[ye[:], idxT[:, j, e_:e_ + 1]] + ([macc[:, :]] if e_ == 0 else []),
                        writes=[], extra_deps=(prev_sc + cur_sc) if e_ == NE - 1 else prev_sc,
                        record_writes=([macc[:, :]] if (e_ == NE - 1 and j == njt - 1) else []))
                    cur_sc.append(ev_)
                prev_sc = cur_sc
                if e_ + 2 < NE:
                    W[(e_ + 2, 2)] = load_w(e_ + 2, 2)
            if "macc" in dbg and lreal == nl - 1:
                for t in range(ntl):
                    x_t = next_xt()
                    kb.dma("sp", x_t[:], macc[t * 128:(t + 1) * 128, :])
                    kb.dma("sp", dbg_out["d_macc"][t * 128:(t + 1) * 128, :], x_t[:])

            load_gate(1)
            li_ = layers.index(lreal)
            nxt_l = layers[li_ + 1] if (not last and li_ + 1 < len(layers) and stop_after is None) else None
            if nxt_l is not None:
                layer(nxt_l, part="p0")
            bcast_row(lngb[:, 0, :], ln_gb[l, 2:3, :])
            bcast_row(lngb[:, 1, :], ln_gb[l, 3:4, :])
            for t in range(ntl):
                s = 0 if t < NTL else 1
                x_old = next_xt()
                kb.dma("sp", x_old[:], xbuf[t * 128:(t + 1) * 128, :])
                x_new = next_xt()
                m_t = next_xt()
                kb.dma("sp", m_t[:], macc[t * 128:(t + 1) * 128, :])
                residual_ln(x_old, (m_t[:, 0:512], m_t[:, 512:1024]), 1, s, x_new)
                if last:
                    kb.dma("pool", y[t * 128:(t + 1) * 128, :], x_new[:])
                else:
                    kb.dma("pool", xbuf[t * 128:(t + 1) * 128, :], x_new[:])
                if nxt_l is not None:
                    xn_f = norm_stage(x_new, t)
                    tr_stage(xn_f, t, 0, dst=hT)
            if nxt_l is not None:
                p1_done.add(nxt_l)

        for nm, shp, dt_ in (("d_hT", [128, 8 * NTOK], BF16), ("d_ya", [128, 2 * NTOK], BF16),
                             ("d_yb", [128, 2 * NTOK], BF16), ("d_yd", [128, 2 * NTOK], BF16),
                             ("d_q", [128, 4 * NTOK], BF16), ("d_k", [128, 4 * NTOK], BF16),
                             ("d_yc", [128, 4 * NTOK], BF16), ("d_mT", [128, 8 * NTOK], BF16),
                             ("d_xmid", [NTOK, D], F32), ("d_aff", [NE, NTOK], F32),
                             ("d_topv", [NE, CAPL + CAPC], F32), ("d_topi", [NE, CAPL + CAPC], F32),
                             ("d_macc", [NTOK, D], F32), ("d_xend", [NTOK, D], F32)):
            if dbg:
                dout(nm, shp, dt_)
        for l in layers:
            if stop_after != "C":
                layer(l)
        if dbg:
            for t in range(NT):
                x_t = next_xt()
                kb.dma("sp", x_t[:], xbuf[t * 128:(t + 1) * 128, :])
                kb.dma("sp", dbg_out["d_xend"][t * 128:(t + 1) * 128, :], x_t[:])
        kb.finish()
        stats = (kb.n_inst, kb.n_wait)
    return nc, dbg_out, stats


def prep_inputs(inputs, layers=None):
    if layers is None:
        layers = list(range(DEPTH))
    f = lambda a: np.ascontiguousarray(np.asarray(a, dtype=np.float32)[layers])
    perm = _w_in_perm()
    shared = {
        "w_ada": f(inputs["w_ada"]), "b_ada": f(inputs["b_ada"]),
        "w_in": f(np.asarray(inputs["w_in"])[:, :, perm]),
        "w_gate": f(np.asarray(inputs["w_gate"]).reshape(DEPTH, D, 4, 8, 128).transpose(0, 3, 1, 2, 4).reshape(DEPTH, 8, D, 512)),
        "b_gate": f(inputs["b_gate"]),
        "sgu_ln_g": f(inputs["sgu_ln_g"]), "sgu_ln_b": f(inputs["sgu_ln_b"]),
        "w_spT": f(np.transpose(np.asarray(inputs["w_sp"]), (0, 1, 3, 2))),
        "b_sp": f(inputs["b_sp"]), "conv_w": f(inputs["conv_w"]),
        "lam_in": f(np.stack([np.asarray(inputs[k]) for k in ("lam_q1", "lam_k1", "lam_q2", "lam_k2")], axis=1)),
        "subln_g": f(inputs["subln_g"]),
        "w_p": f(np.concatenate([np.asarray(inputs[k]) for k in ("w_pa", "w_pb", "w_pd", "w_pc")], axis=1)
                 .reshape(DEPTH, 1280, 8, 128).transpose(0, 2, 1, 3)),
        "w_o": f(inputs["w_o"]),
        "ln_gb": f(np.stack([np.asarray(inputs[k]) for k in ("ln1_g", "ln1_b", "ln2_g", "ln2_b")], axis=1)),
        "w_router": f(inputs["w_router"]),
        "w_exp_gate": f(inputs["w_exp_gate"]), "w_exp_up": f(inputs["w_exp_up"]),
        "w_exp_down": f(inputs["w_exp_down"]),
    }
    shared.update(_get_consts())
    x = np.asarray(inputs["x"], dtype=np.float32)
    ctx = np.asarray(inputs["ctx"], dtype=np.float32)
    c = np.asarray(inputs["c"], dtype=np.float32)
    c_ctx = np.asarray(inputs["c_ctx"], dtype=np.float32)
    in_maps = []
    for b in range(x.shape[0]):
        m = dict(shared)
        m["xin"] = np.ascontiguousarray(np.concatenate([x[b], ctx[b]], axis=0))
        m["cc"] = np.ascontiguousarray(np.stack([c[b], c_ctx], axis=0))
        in_maps.append(m)
    return in_maps


_PROG = {}


def kernel(**inputs):
    in_maps = prep_inputs(inputs)
    if "nc" not in _PROG:
        _PROG["nc"] = build()[0]
    nc = _PROG["nc"]
    res = run_bass_kernel_spmd(nc, in_maps, core_ids=list(range(len(in_maps))))
    out = np.stack([np.asarray(r["y"], dtype=np.float32) for r in res.results], axis=0)
    return out
```

```python
import math
from contextlib import ExitStack

import numpy as np
import ml_dtypes
import concourse.bass as bass
import concourse.mybir as mybir
from concourse.bass_utils import run_bass_kernel_spmd

F32 = mybir.dt.float32
BF16 = mybir.dt.bfloat16
I32 = mybir.dt.int32
U32 = mybir.dt.uint32
AF = mybir.ActivationFunctionType
ALU = mybir.AluOpType
AX = mybir.AxisListType

D = 1024
SEQ = 2048
CTX = 256
NTOK = SEQ + CTX
NT = NTOK // 128
NTL = SEQ // 128
DEPTH = 4
NE = 16
CAPL = 256
CAPC = 32
DN_ALPHA = (2 * DEPTH) ** 0.25
LN_EPS = 1e-5
RMS_EPS = 1e-5
TB = [(i * 512, 512) for i in range(4)] + [(2048, 256)]
_ESZ = {}


def _esize(dt):
    s = _ESZ.get(dt)
    if s is None:
        s = 2 if dt == BF16 else 4
        _ESZ[dt] = s
    return s


class Ev:
    __slots__ = ("key", "val", "clock")

    def __init__(self, key, val, clock):
        self.key = key
        self.val = val
        self.clock = clock


def _box(ap):
    t = ap.tensor
    name = t.name
    aps = ap.ap
    off = ap.offset
    es = _esize(ap.dtype)
    tn = type(t).__name__
    if "PSum" in tn:
        return (name, 0, 128, 0, 2048, True)
    if "DRam" in tn:
        ext = 0
        for st, cnt in aps:
            ext += abs(st) * (cnt - 1)
        return (name, 0, 1, off * es, (off + ext + 1) * es)
    pstep = aps[0][0]
    if pstep == 0:
        p0 = 0
        f0 = off
    else:
        p0 = off // pstep
        f0 = off - p0 * pstep
    pcnt = aps[0][1]
    ext = 0
    for st, cnt in aps[1:]:
        ext += abs(st) * (cnt - 1)
    return (name, p0, p0 + pcnt, f0 * es, (f0 + ext + 1) * es)


class KB:
    def __init__(self, nc, dma_ring=8):
        self.nc = nc
        self.eng = {"pe": nc.tensor, "act": nc.scalar, "dve": nc.vector, "pool": nc.gpsimd, "sp": nc.sync}
        self.sems = {}
        self.cnt = {}
        for e in self.eng:
            self.sems[e] = nc.alloc_semaphore(name=f"s_{e}")
            self.cnt[e] = 0
        self.known = {e: {} for e in self.eng}
        self.ring = {}
        self.ring_i = {}
        for q in ("sp", "pool", "act"):
            self.ring[q] = []
            for i in range(dma_ring):
                key = f"d_{q}{i}"
                self.sems[key] = nc.alloc_semaphore(name=key)
                self.cnt[key] = 0
                self.ring[q].append(key)
            self.ring_i[q] = 0
        self.acc = {}
        self.alias = {}
        self.n_inst = 0
        self.n_wait = 0

    def _name(self, b):
        return self.alias.get(b[0], b[0])

    def _deps(self, reads, writes):
        deps = []
        for ap in reads:
            b = _box(ap)
            d = self.acc.get(self._name(b))
            if not d:
                continue
            ps_ = len(b) > 5
            for (ob, w, key), ev in d.items():
                if (w or ps_) and ob[1] < b[2] and b[1] < ob[2] and ob[3] < b[4] and b[3] < ob[4]:
                    deps.append(ev)
        for ap in writes:
            b = _box(ap)
            d = self.acc.get(self._name(b))
            if not d:
                continue
            for (ob, w, key), ev in d.items():
                if ob[1] < b[2] and b[1] < ob[2] and ob[3] < b[4] and b[3] < ob[4]:
                    deps.append(ev)
        return deps

    def _record(self, reads, writes, ev):
        for ap in reads:
            b = _box(ap)
            d = self.acc.setdefault(self._name(b), {})
            d[(b, len(b) > 5, ev.key)] = ev
        for ap in writes:
            b = _box(ap)
            d = self.acc.setdefault(self._name(b), {})
            if len(b) <= 5:
                dead = [k for k in d
                        if k[0][1] >= b[1] and k[0][2] <= b[2] and k[0][3] >= b[3] and k[0][4] <= b[4]]
                for k in dead:
                    del d[k]
            d[(b, True, ev.key)] = ev

    def _wait_for(self, e, deps, skip_same=False):
        kn = self.known[e]
        need = {}
        for ev in deps:
            if skip_same and ev.key == e:
                continue
            if kn.get(ev.key, 0) >= ev.val:
                continue
            if need.get(ev.key, (0, None))[0] < ev.val:
                need[ev.key] = (ev.val, ev)
        items = sorted(need.items(), key=lambda kv: -len(kv[1][1].clock))
        for key, (val, ev) in items:
            if kn.get(key, 0) >= val:
                continue
            self.eng[e].wait_ge(self.sems[key], val)
            self.n_wait += 1
            for k2, v2 in ev.clock.items():
                if kn.get(k2, 0) < v2:
                    kn[k2] = v2
            if kn.get(key, 0) < val:
                kn[key] = val

    def op(self, e, fn, reads=(), writes=()):
        deps = self._deps(reads, writes)
        self._wait_for(e, deps, skip_same=(e == "pe"))
        ins = fn(self.eng[e])
        self.cnt[e] += 1
        v = self.cnt[e]
        ins.then_inc(self.sems[e], 1)
        clock = dict(self.known[e])
        clock[e] = v
        ev = Ev(e, v, clock)
        self._record(reads, writes, ev)
        self.n_inst += 1
        return ev

    def dma_raw(self, q, fn, reads, writes, extra_deps=(), record_writes=None):
        deps = self._deps(reads, writes)
        deps.extend(extra_deps)
        i = self.ring_i[q]
        self.ring_i[q] = i + 1
        key = self.ring[q][i % len(self.ring[q])]
        prev = self.cnt[key]
        if prev and self.known[q].get(key, 0) < prev:
            self.eng[q].wait_ge(self.sems[key], prev)
            self.known[q][key] = prev
            self.n_wait += 1
        self._wait_for(q, deps)
        ins = fn(self.eng[q])
        ins.then_inc(self.sems[key], 16)
        self.cnt[key] = prev + 16
        clock = dict(self.known[q])
        clock[key] = prev + 16
        ev = Ev(key, prev + 16, clock)
        self._record(reads, writes if record_writes is None else record_writes, ev)
        self.n_inst += 1
        return ev

    def dma(self, q, out, in_, **kw):
        return self.dma_raw(q, lambda e: e.dma_start(out=out, in_=in_, **kw), [in_], [out])

    def finish(self):
        e = "sp"
        for key, c in self.cnt.items():
            if c and self.known[e].get(key, 0) < c:
                self.eng[e].wait_ge(self.sems[key], c)
                self.known[e][key] = c


def _w_in_perm():
    perm = []
    perm += list(range(0, 512))
    for f in range(2):
        perm += list(range(768 + f * 128, 768 + (f + 1) * 128))
        perm += list(range(1024 + f * 128, 1024 + (f + 1) * 128))
        perm += list(range(512 + f * 128, 512 + (f + 1) * 128))
        perm += list(range(2816 + f * 128, 2816 + (f + 1) * 128))
    for base0 in (1280, 1792):
        de, sw = [], []
        for blk in range(8):
            b = base0 + blk * 64
            ev = [b + 2 * j for j in range(32)]
            od = [b + 2 * j + 1 for j in range(32)]
            de += ev + od
            sw += od + ev
        perm += de + sw
    perm += list(range(2304, 2816))
    return np.array(perm, dtype=np.int64)


def _consts():
    c = {}
    c["ident"] = np.eye(128, dtype=np.float32)
    t = np.arange(SEQ)
    row = (t // 64).astype(np.float32)
    col = (t % 64).astype(np.float32)
    inv = (10000.0 ** (-np.arange(0, 32, 2, dtype=np.float32) / 32)).astype(np.float32)
    ang = np.concatenate([row[:, None] * inv, col[:, None] * inv], axis=-1).astype(np.float32)
    cs = np.cos(ang).astype(np.float32)
    sn = np.sin(ang).astype(np.float32)
    cosT = np.ones((128, NTOK), np.float32)
    sinT = np.zeros((128, NTOK), np.float32)
    for p in range(128):
        r = p % 64
        j = r % 32
        cosT[p, :SEQ] = cs[:, j]
        sinT[p, :SEQ] = -sn[:, j] if r < 32 else sn[:, j]
    c["ropecos"] = cosT
    c["ropesin"] = sinT
    def dft(n):
        k = np.arange(n, dtype=np.float64)
        a = 2 * np.pi * np.outer(k, k) / n
        return np.cos(a), np.sin(a)
    C, S = dft(SEQ)
    c["dftC"] = (C / math.sqrt(SEQ)).astype(ml_dtypes.bfloat16)
    c["dftS"] = (-S / math.sqrt(SEQ)).astype(ml_dtypes.bfloat16)
    C, S = dft(CTX)
    c["dftCc"] = (C / math.sqrt(CTX)).astype(ml_dtypes.bfloat16)
    c["dftSc"] = (-S / math.sqrt(CTX)).astype(ml_dtypes.bfloat16)
    C, S = dft(64)
    c64 = np.zeros((128, 256), np.float32)
    for g in range(2):
        c64[g * 64:(g + 1) * 64, g * 64:(g + 1) * 64] = C / 8.0
        c64[g * 64:(g + 1) * 64, 128 + g * 64:128 + (g + 1) * 64] = S / 8.0
    c["c64"] = c64.astype(ml_dtypes.bfloat16)
    c["zeros"] = np.zeros((128, D), np.float32)
    return c


_CONST_CACHE = {}


def _get_consts():
    if not _CONST_CACHE:
        _CONST_CACHE.update(_consts())
    return _CONST_CACHE


def build(layers=None, dbg=(), stop_after=None):
    nc = bass.Bass("TRN2", target_bir_lowering=False)
    if layers is None:
        layers = list(range(DEPTH))
    L = len(layers)
    nl = layers[-1] + 1

    def din(name, shape, dt=F32):
        return nc.dram_tensor(name, list(shape), dt, kind="ExternalInput").ap()

    xin = din("xin", [NTOK, D])
    cc = din("cc", [2, D])
    w_ada = din("w_ada", [L, D, 6 * D])
    b_ada = din("b_ada", [L, 6 * D])
    w_in = din("w_in", [L, D, 4096])
    w_gate = din("w_gate", [L, 8, D, 512])
    b_gate = din("b_gate", [L, 4 * D])
    sgu_g = din("sgu_ln_g", [L, 256])
    sgu_b = din("sgu_ln_b", [L, 256])
    w_spT = din("w_spT", [L, 4, 128, 128])
    b_sp = din("b_sp", [L, 4, 128])
    conv_w = din("conv_w", [L, 3, 256])
    lam_in = din("lam_in", [L, 4, 64])
    subln_g = din("subln_g", [L, 128])
    w_p = din("w_p", [L, 8, 1280, 128])
    w_o = din("w_o", [L, D, D])
    ln_gb = din("ln_gb", [L, 4, D])
    w_router = din("w_router", [L, D, NE])
    w_eg = din("w_exp_gate", [L, NE, D, D])
    w_eu = din("w_exp_up", [L, NE, D, D])
    w_ed = din("w_exp_down", [L, NE, D, D])
    ident_d = din("ident", [128, 128])
    ropecos = din("ropecos", [128, NTOK])
    ropesin = din("ropesin", [128, NTOK])
    dftC = din("dftC", [SEQ, SEQ], BF16)
    dftS = din("dftS", [SEQ, SEQ], BF16)
    dftCc = din("dftCc", [CTX, CTX], BF16)
    dftSc = din("dftSc", [CTX, CTX], BF16)
    c64_d = din("c64", [128, 256], BF16)
    zeros_d = din("zeros", [128, D])

    y = nc.dram_tensor("y", [SEQ, D], F32, kind="ExternalOutput").ap()
    xbuf = nc.dram_tensor("xbuf", [NTOK, D], F32, kind="Internal").ap()
    xn2 = nc.dram_tensor("xn2", [NTOK + 128, D], F32, kind="Internal").ap()
    macc = nc.dram_tensor("macc", [NTOK + 128, D], F32, kind="Internal").ap()
    dbg_out = {}

    def dout(name, shape, dt=F32):
        t = nc.dram_tensor(name, list(shape), dt, kind="ExternalOutput").ap()
        dbg_out[name] = t
        return t

    es = ExitStack()
    with es:
        kb = KB(nc)

        def sb(name, shape, dt=F32):
            return es.enter_context(nc.sbuf_tensor(name, list(shape), dt))

        banks = [es.enter_context(nc.psum_tensor(f"ps{i}", [128, 512], F32)) for i in range(8)]
        bank_i = [0]
        held = set()

        def ps(hold=False):
            while True:
                i = bank_i[0] % 8
                bank_i[0] += 1
                if i not in held:
                    break
            if hold:
                held.add(i)
            return banks[i]

        def release(b):
            held.discard(banks.index(b))

        def mm(out, lhsT, rhs, start=True, stop=True, **kw):
            return kb.op("pe", lambda e: e.matmul(out, lhsT, rhs, start=start, stop=stop, **kw),
                         reads=[lhsT, rhs], writes=[out])

        def tr(out, in_):
            k_ = in_.shape[0]
            return kb.op("pe", lambda e: e.transpose(out, in_, ident[0:k_, 0:k_]), reads=[in_, ident[0:k_, 0:k_]],
                         writes=[out])

        def act(out, in_, func, bias=None, scale=None, accum_out=None, eng="act"):
            reads = [in_]
            kw = {}
            if bias is not None:
                kw["bias"] = bias
                if not isinstance(bias, float):
                    reads.append(bias)
            if scale is not None:
                kw["scale"] = scale
                if not isinstance(scale, float):
                    reads.append(scale)
            writes = [out]
            if accum_out is not None:
                kw["accum_out"] = accum_out
                writes.append(accum_out)
            return kb.op("act", lambda e: e.activation(out, in_, func, **kw), reads=reads, writes=writes)

        def tt(eng, out, in0, in1, op):
            if eng == "pool":
                eng = "dve"
            return kb.op(eng, lambda e: e.tensor_tensor(out, in0, in1, op), reads=[in0, in1], writes=[out])

        def tt_pool(out, in0, in1, op):
            return kb.op("pool", lambda e: e.tensor_tensor(out, in0, in1, op), reads=[in0, in1], writes=[out])

        def ts(eng, out, in0, s1, s2, op0, op1=None, accum_out=None):
            if eng == "pool":
                eng = "dve"
            reads = [in0]
            for s in (s1, s2):
                if s is not None and not isinstance(s, (float, int)):
                    reads.append(s)
            writes = [out]
            kw = {}
            if op1 is not None:
                kw["op1"] = op1
            if accum_out is not None:
                kw["accum_out"] = accum_out
                writes.append(accum_out)
            return kb.op(eng, lambda e: e.tensor_scalar(out, in0, s1, s2, op0, **kw), reads=reads, writes=writes)

        def stt(eng, out, in0, scalar, in1, op0, op1):
            eng = "dve"
            reads = [in0, in1]
            if not isinstance(scalar, (float, int)):
                reads.append(scalar)
            return kb.op(eng, lambda e: e.scalar_tensor_tensor(out, in0, scalar, in1, op0, op1),
                         reads=reads, writes=[out])

        def cp(eng, out, in_):
            if eng == "pool":
                eng = "dve"
            if eng == "act":
                return kb.op("act", lambda e: e.copy(out, in_), reads=[in_], writes=[out])
            return kb.op(eng, lambda e: e.tensor_copy(out, in_), reads=[in_], writes=[out])

        def memset(eng, ap, val):
            if eng == "pool":
                eng = "dve"
            return kb.op(eng, lambda e: e.memset(ap, val), reads=[], writes=[ap])

        ident = sb("ident_sb", [128, 128])
        c64 = sb("c64_sb", [128, 256], BF16)
        hT = sb("hT", [128, 8, NTOK], BF16)
        ARENA = 44032
        arena = sb("arena", [128, ARENA], BF16)
        ybr = arena[:, 0:23040].rearrange("p (a b) -> p a b", a=10)
        MT0 = 23040
        mT = arena[:, MT0:MT0 + 18432].rearrange("p (a b) -> p a b", a=8)
        kT = arena[:, MT0:MT0 + 9216].rearrange("p (a b) -> p a b", a=4)
        vaug = arena[:, MT0 + 9216:MT0 + 9216 + 9288].rearrange("p (t h e) -> p t h e", t=NT, h=4)
        QT0 = MT0 + 18560
        qblk = arena[:, QT0:QT0 + 2048].rearrange("p (a b) -> p a b", a=4)
        ycv = arena[:, MT0:MT0 + 9216].bitcast(F32)
        AB = arena[:, MT0 + 9216:MT0 + 18432].rearrange("p (t c) -> p t c", t=NT)
        xnT_r = arena[:, 0:2048].bitcast(F32).rearrange("p (k n) -> p k n", k=8)
        affT = arena[0:NE, 2048:2048 + 4608].bitcast(F32)
        affW = arena[0:NE, 2048 + 4608:2048 + 9216].bitcast(F32)
        topv = arena[0:NE, 0:576].bitcast(F32)
        topi = arena[0:NE, 576:1152].bitcast(U32)
        topif = arena[0:NE, 1152:1728].bitcast(F32)
        xeT = arena[:, 2048:2048 + 2304].rearrange("p (k n) -> p k n", k=8)
        hidT = arena[:, 2048 + 2304:2048 + 4608].rearrange("p (k n) -> p k n", k=8)
        EW0 = 11264
        ewslot = [arena[:, EW0 + i * 8192:EW0 + (i + 1) * 8192].rearrange("p (k n) -> p k n", k=8) for i in range(4)]
        hT_flat = hT[:].rearrange("p a b -> p (a b)")
        ewslot += [hT_flat[:, i * 8192:(i + 1) * 8192].rearrange("p (k n) -> p k n", k=8) for i in range(2)]
        ew_i = [0]

        def next_ew():
            w = ewslot[ew_i[0] % 6]
            ew_i[0] += 1
            return w

        NW = 3
        wbuf = [sb(f"wbuf{i}", [128, 8, 512], BF16) for i in range(NW)]
        wb_i = [0]

        def next_wbuf():
            w = wbuf[wb_i[0] % NW]
            wb_i[0] += 1
            return w

        xt = [sb(f"xt{i}", [128, D]) for i in range(3)]
        xt_i = [0]

        def next_xt():
            t = xt[xt_i[0] % 3]
            xt_i[0] += 1
            return t

        tmpa = [sb(f"tmpa{i}", [128, 512]) for i in range(4)]
        tmpa_i = [0]

        def next_tmp():
            t = tmpa[tmpa_i[0] % 4]
            tmpa_i[0] += 1
            return t

        ebuf = [sb(f"ebuf{i}", [128, 512], BF16) for i in range(4)]
        ebuf_i = [0]

        def next_e():
            t = ebuf[ebuf_i[0] % 4]
            ebuf_i[0] += 1
            return t

        stat = sb("stat", [128, 64])
        stat_i = [0]

        def next_stat(n):
            if stat_i[0] + n > 64:
                stat_i[0] = 0
            s_ = stat[:, stat_i[0]:stat_i[0] + n]
            stat_i[0] += n
            return s_

        sc_col = sb("sc_col", [128, 8, 2], BF16)
        c_col = sb("c_col", [128, 8, 2])
        modT = sb("modT", [128, 4, 8, 2])
        badaT = sb("badaT", [128, 48])
        gate_bc = sb("gate_bc", [128, 2, D])
        lngb = sb("lngb", [128, 2, D])
        sgugb = sb("sgugb", [128, 2, 256])
        wsp = sb("wsp", [128, 4, 128], BF16)
        bsp_row = sb("bsp_row", [1, 512])
        ones_row = sb("ones_row", [1, 128])
        convT = sb("convT", [128, 2, 3])
        conv_rows = sb("conv_rows", [6, 128])
        bgT = sb("bgT", [128, 32])
        rows48 = sb("rows48", [48, 128])
        lam_t = sb("lam_t", [128, 4, 64])
        lam_s = sb("lam_s", [128, 4])
        subg = sb("subg", [128, 128])
        wpb = [sb("wpb0", [128, 10, 128], BF16), sb("wpb1", [128, 10, 128], BF16)]
        wr_sb = sb("wr_sb", [128, 8, NE])
        wr_mod = sb("wr_mod", [128, 8, 2, NE])
        rb_bc = sb("rb_bc", [128, 2, NE])
        idxT = sb("idxT", [128, 3, NE], U32)
        eps_t = sb("eps_t", [128, 1])
        gatT = sb("gatT", [128, 3, NE])

        kb.dma("sp", ident[:], ident_d)
        kb.dma("sp", c64[:], c64_d)
        memset("pool", ones_row[:], 1.0)
        memset("pool", eps_t[:], LN_EPS)
        for (p0_, p1_) in ((32, 64), (64, 128)):
            kb.op("pool", lambda e: e.iota(idxT[p0_:p1_, 2, :], [[0, NE]], base=NTOK + p0_, channel_multiplier=1),
                  reads=[], writes=[idxT[p0_:p1_, 2, :]])
        for t in range(NT):
            x_t = next_xt()
            kb.dma("sp", x_t[:], xin[t * 128:(t + 1) * 128, :])
            kb.dma("sp", xbuf[t * 128:(t + 1) * 128, :], x_t[:])
        with nc.allow_non_contiguous_dma(reason="tiny column loads"):
            for s in range(2):
                kb.dma("sp", c_col[:, :, s], cc[s, :].rearrange("(a p) -> p a", p=128))
        act(sc_col[:], c_col[:], AF.Silu)

        def load_rows_T(dst_cols, src_rows_ap, nrows, stage):
            kb.dma("sp", stage[0:nrows, :], src_rows_ap)
            p = ps()
            tr(p[:, 0:nrows], stage[0:nrows, :])
            cp("dve", dst_cols, p[:, 0:nrows])

        def bcast_row(dst, src_row_ap):
            kb.dma("sp", dst, src_row_ap.partition_broadcast(128))

        p0_done = set()
        p1_done = set()

        def layer(lreal, part="all"):
            last = (lreal == DEPTH - 1)
            lam_init = 0.8 - 0.6 * math.exp(-0.3 * lreal)
            l = layers.index(lreal)
            ntl = NTL if last else NT
            tbl = TB[:4] if last else TB

            def load_gate(gi):
                for half in range(2):
                    j = (4 if gi == 0 else 10) + half
                    wb = next_wbuf()
                    kb.dma("pool", wb[:], w_ada[l, :, j * 512:(j + 1) * 512].rearrange("(k p) n -> p k n", p=128))
                    brow = next_tmp()
                    bcast_row(brow[:], b_ada[l:l + 1, j * 512:(j + 1) * 512])
                    for s in range(2):
                        p = ps()
                        for kc in range(8):
                            mm(p[:], sc_col[:, kc, s:s + 1].to_broadcast([128, 128]), wb[:, kc, :],
                               start=(kc == 0), stop=(kc == 7))
                        tt("dve", gate_bc[:, s, half * 512:(half + 1) * 512], p[:], brow[:], ALU.add)

            if lreal not in p0_done:
                load_rows_T(badaT[:], b_ada[l].rearrange("(a p) -> a p", p=128), 48, rows48)
                for j in (0, 1, 2, 3, 6, 7, 8, 9):
                    wb = next_wbuf()
                    kb.dma("pool", wb[:], w_ada[l, :, j * 512:(j + 1) * 512].rearrange("(k p) n -> p k n", p=128))
                    vec = j // 2
                    mi = {0: 0, 1: 1, 3: 2, 4: 3}[vec]
                    p = ps()
                    import os
                    if os.environ.get("KDBG") == "nomm":
                        continue
                    for sub in range(4):
                        for kc in range(8):
                            mm(p[:, sub * 2:sub * 2 + 2], wb[:, kc, sub * 128:(sub + 1) * 128], sc_col[:, kc, :],
                               start=(kc == 0), stop=(kc == 7))
                    if os.environ.get("KDBG") == "noevac":
                        continue
                    for sub in range(4):
                        dc = (j % 2) * 4 + sub
                        col = j * 4 + sub
                        if mi in (1, 3):
                            ts("dve", modT[:, mi, dc, :], p[:, sub * 2:sub * 2 + 2], badaT[:, col:col + 1], 1.0,
                               ALU.add, ALU.add)
                        else:
                            ts("dve", modT[:, mi, dc, :], p[:, sub * 2:sub * 2 + 2], badaT[:, col:col + 1], None,
                               ALU.add)
                bcast_row(sgugb[:, 0, :], sgu_g[l:l + 1, :])
                bcast_row(sgugb[:, 1, :], sgu_b[l:l + 1, :])
                kb.dma("pool", wsp[:], w_spT[l].rearrange("g q p -> q g p"))
                kb.dma("sp", bsp_row[:], b_sp[l:l + 1].rearrange("o g p -> o (g p)"))
                kb.dma("sp", conv_rows[:], conv_w[l].rearrange("k (f p) -> (k f) p", p=128))
                p = ps()
                tr(p[:, 0:6], conv_rows[:])
                cp("dve", convT[:].rearrange("p f k -> p k f"), p[:, 0:6].rearrange("p (k f) -> p k f", k=3))
                load_rows_T(bgT[:], b_gate[l].rearrange("(a p) -> a p", p=128), 32, rows48)
                bcast_row(subg[:], subln_g[l:l + 1, :])
                ts("pool", subg[:], subg[:], 1.0 - lam_init, None, ALU.mult)
                kb.dma("sp", lam_t[:].rearrange("p a b -> p (a b)"),
                       lam_in[l:l + 1].rearrange("o a b -> o (a b)").partition_broadcast(128))
                lt = next_tmp()
                for i in range(2):
                    tt("dve", lt[:, i * 64:(i + 1) * 64], lam_t[:, 2 * i, :], lam_t[:, 2 * i + 1, :], ALU.mult)
                    kb.op("dve", lambda e: e.reduce_sum(lam_s[:, i:i + 1], lt[:, i * 64:(i + 1) * 64], AX.X),
                          reads=[lt[:, i * 64:(i + 1) * 64]], writes=[lam_s[:, i:i + 1]])
                act(lam_s[:, 0:2], lam_s[:, 0:2], AF.Exp)
                tt("dve", lam_s[:, 2:3], lam_s[:, 0:1], lam_s[:, 1:2], ALU.subtract)
                ts("dve", lam_s[:, 3:4], lam_s[:, 2:3], lam_init, -1.0, ALU.add, ALU.mult)

                p0_done.add(lreal)
            if part == "p0":
                return
            if stop_after == "P0":
                return

            def ln_stats(x_ap, width):
                nchunk = (width + 511) // 512
                st = next_stat(16)
                bst = next_tmp()
                for c_ in range(nchunk):
                    w_ = min(512, width - c_ * 512)
                    kb.op("dve", lambda e: e.bn_stats(bst[:, c_ * 6:(c_ + 1) * 6], x_ap[:, c_ * 512:c_ * 512 + w_]),
                          reads=[x_ap[:, c_ * 512:c_ * 512 + w_]], writes=[bst[:, c_ * 6:(c_ + 1) * 6]])
                kb.op("dve", lambda e: e.bn_aggr(st[:, 0:2], bst[:, 0:nchunk * 6]),
                      reads=[bst[:, 0:nchunk * 6]], writes=[st[:, 0:2]])
                act(st[:, 2:3], st[:, 1:2], AF.Ln, bias=eps_t[:, 0:1])
                act(st[:, 2:3], st[:, 2:3], AF.Exp, scale=-0.5)
                stt("dve", st[:, 3:4], st[:, 0:1], -1.0, st[:, 2:3], ALU.mult, ALU.mult)
                ln_stats.nbias = st[:, 3:4]
                return st[:, 0:1], st[:, 2:3]

            def normalize_to_hT(x_t, t, mi_shift, dst=None, also_xn2=False, router=False):
                xn = norm_stage(x_t, t, also_xn2)
                tr_stage(xn, t, mi_shift, dst, router)
                return xn

            def norm_stage(x_t, t, also_xn2=False):
                mean, rstd = ln_stats(x_t, D)
                xn = x_t
                act(xn[:], x_t[:], AF.Identity, bias=ln_stats.nbias, scale=rstd)
                if also_xn2:
                    kb.dma("pool", xn2[t * 128:(t + 1) * 128, :], xn[:])
                return xn

            def tr_stage(xn, t, mi_shift, dst=None, router=False):
                s = 0 if t < NTL else 1
                for half in range(2):
                    p = ps()
                    for k4 in range(4):
                        kc = half * 4 + k4
                        tr(p[:, k4 * 128:(k4 + 1) * 128], xn[:, kc * 128:(kc + 1) * 128])
                    for k4 in range(4):
                        kc = half * 4 + k4
                        if dst is not None:
                            if half == 0:
                                act(dst[:, kc, t * 128:(t + 1) * 128], p[:, k4 * 128:(k4 + 1) * 128], AF.Identity,
                                    bias=modT[:, mi_shift, kc, s:s + 1], scale=modT[:, mi_shift + 1, kc, s:s + 1])
                            else:
                                ts("dve", dst[:, kc, t * 128:(t + 1) * 128], p[:, k4 * 128:(k4 + 1) * 128],
                                   modT[:, mi_shift + 1, kc, s:s + 1], modT[:, mi_shift, kc, s:s + 1],
                                   ALU.mult, ALU.add)
                        if router:
                            xr = xnT_r[:, kc, :]
                            if half == 1:
                                cp("dve", xr, p[:, k4 * 128:(k4 + 1) * 128])
                            else:
                                cp("act", xr, p[:, k4 * 128:(k4 + 1) * 128])

            def p1_load_norm(t):
                x_t = next_xt()
                kb.dma("sp", x_t[:], xbuf[t * 128:(t + 1) * 128, :])
                return norm_stage(x_t, t)

            if lreal not in p1_done:
                xn_cur = p1_load_norm(0)
                for t in range(NT):
                    xn_nxt = p1_load_norm(t + 1) if t + 1 < NT else None
                    tr_stage(xn_cur, t, 0, dst=hT)
                    xn_cur = xn_nxt
            if "hT" in dbg and lreal == nl - 1:
                kb.dma("sp", dbg_out["d_hT"].rearrange("p (a b) -> p a b", a=8), hT[:])
            if stop_after == "P1":
                return

            def load_win(ci):
                wb = next_wbuf()
                kb.dma("pool", wb[:], w_in[l, :, ci * 512:(ci + 1) * 512].rearrange("(k p) n -> p k n", p=128))
                return wb

            def fm_group(p_ap, wb, col0, tok0, ntok, ncol=128):
                for kc in range(8):
                    mm(p_ap, wb[:, kc, col0:col0 + ncol], hT[:, kc, tok0:tok0 + ntok], start=(kc == 0), stop=(kc == 7))

            def tm_group(p_ap, wb, col0, ncol, t):
                for kc in range(8):
                    mm(p_ap, hT[:, kc, t * 128:(t + 1) * 128], wb[:, kc, col0:col0 + ncol],
                       start=(kc == 0), stop=(kc == 7))

            wb = load_win(0)
            for fc in range(2):
                for (t0, n) in TB:
                    p = ps()
                    fm_group(p[:, 0:n], wb, fc * 128, t0, n)
                    act(ybr[:, fc, t0:t0 + n], p[:, 0:n], AF.Gelu_apprx_tanh)
            for t in range(NT):
                p = ps()
                tm_group(p[:, 0:256], wb, 256, 256, t)
                vt = next_tmp()
                act(vt[:, 0:256], p[:, 0:256], AF.Gelu_apprx_tanh)
                mean, rstd = ln_stats(vt[:, 0:256], 256)
                ts("dve", vt[:, 0:256], vt[:, 0:256], mean, rstd, ALU.subtract, ALU.mult)
                tt("pool", vt[:, 0:256], vt[:, 0:256], sgugb[:, 0, :], ALU.mult)
                vn = next_e()
                tt("pool", vn[:, 0:256], vt[:, 0:256], sgugb[:, 1, :], ALU.add)
                p2 = ps()
                for gp in range(2):
                    for g2 in range(2):
                        g = gp * 2 + g2
                        o = p2[g2 * 64:(g2 + 1) * 64, gp * 128:(gp + 1) * 128]
                        mm(o, vn[:, g * 64:(g + 1) * 64], wsp[:, g, :], start=True, stop=False)
                        mm(o, ones_row[0:1, 0:64], bsp_row[0:1, g * 128:(g + 1) * 128], start=False, stop=True)
                for gp in range(2):
                    tt("dve", ybr[:, gp, t * 128:(t + 1) * 128], ybr[:, gp, t * 128:(t + 1) * 128],
                       p2[:, gp * 128:(gp + 1) * 128], ALU.mult)
            if "ya" in dbg and lreal == nl - 1:
                kb.dma("sp", dbg_out["d_ya"].rearrange("p (a b) -> p a b", a=2), ybr[:, 0:2, :])
            if stop_after == "ya":
                return

            ybuf = ycv[:, 0:NTOK]
            cvbuf = ycv[:, NTOK:2 * NTOK]
            for f in range(2):
                wb = load_win(1 + f)
                for (t0, n) in TB:
                    pa = ps()
                    fm_group(pa[:, 0:n], wb, 0, t0, n)
                    pb = ps()
                    fm_group(pb[:, 0:n], wb, 128, t0, n)
                    tg = next_tmp()
                    cp("act", tg[:, 0:n], pa[:, 0:n])
                    tt("dve", ybuf[:, t0:t0 + n], tg[:, 0:n], pb[:, 0:n], ALU.mult)
                for (t0, n) in TB:
                    pz = ps()
                    fm_group(pz[:, 0:n], wb, 384, t0, n)
                    cp("act", ybr[:, 4 + f, t0:t0 + n], pz[:, 0:n])
                ts("pool", cvbuf, ybuf, convT[:, f, 1:2], None, ALU.mult)
                for (s0, n) in ((0, SEQ), (SEQ, CTX)):
                    stt("pool", cvbuf[:, s0 + 1:s0 + n], ybuf[:, s0:s0 + n - 1], convT[:, f, 0:1],
                        cvbuf[:, s0 + 1:s0 + n], ALU.mult, ALU.add)
                    stt("pool", cvbuf[:, s0:s0 + n - 1], ybuf[:, s0 + 1:s0 + n], convT[:, f, 2:3],
                        cvbuf[:, s0:s0 + n - 1], ALU.mult, ALU.add)
                for (t0, n) in TB:
                    pg = ps()
                    fm_group(pg[:, 0:n], wb, 256, t0, n)
                    tt("dve", ybr[:, 2 + f, t0:t0 + n], pg[:, 0:n], cvbuf[:, t0:t0 + n], ALU.mult)
            if "yb" in dbg and lreal == nl - 1:
                kb.dma("sp", dbg_out["d_yb"].rearrange("p (a b) -> p a b", a=2), ybr[:, 2:4, :])

            for t in range(NT):
                p = ps()
                for f in range(2):
                    for cs_ in range(2):
                        mm(p[:, cs_ * 256 + f * 128: cs_ * 256 + (f + 1) * 128], ybr[:, 4 + f, t * 128:(t + 1) * 128],
                           c64[:, cs_ * 128:(cs_ + 1) * 128], start=True, stop=True)
                if t % 2 == 0:
                    cp("act", AB[:, t, :], p[:])
                else:
                    cp("dve", AB[:, t, :], p[:])
            pacc = {}
            for nb in range(4):
                for f in range(2):
                    pacc[(nb, f)] = ps(hold=True)
            for nt_ in range(NTL):
                wc = next_wbuf()
                wc_v = wc[:].rearrange("p k n -> p (k n)")
                kb.dma("sp", wc_v[:, 0:2048], dftC[nt_ * 128:(nt_ + 1) * 128, :])
                kb.dma("sp", wc_v[:, 2048:4096], dftS[nt_ * 128:(nt_ + 1) * 128, :])
                for nb in range(4):
                    for f in range(2):
                        mm(pacc[(nb, f)][:], AB[:, nt_, f * 128:(f + 1) * 128], wc_v[:, nb * 512:(nb + 1) * 512],
                           start=(nt_ == 0), stop=False)
                        mm(pacc[(nb, f)][:], AB[:, nt_, 256 + f * 128:256 + (f + 1) * 128],
                           wc_v[:, 2048 + nb * 512:2048 + (nb + 1) * 512], start=False, stop=(nt_ == NTL - 1))
            for nb in range(4):
                for f in range(2):
                    if f == 0:
                        cp("act", ybr[:, 4 + f, nb * 512:(nb + 1) * 512], pacc[(nb, f)][:])
                    else:
                        cp("dve", ybr[:, 4 + f, nb * 512:(nb + 1) * 512], pacc[(nb, f)][:])
            for v_ in pacc.values():
                release(v_)
            wc = next_wbuf()
            wc_v = wc[:].rearrange("p k n -> p (k n)")
            for c2 in range(2):
                kb.dma("sp", wc_v[:, c2 * 512:c2 * 512 + 256], dftCc[c2 * 128:(c2 + 1) * 128, :])
                kb.dma("sp", wc_v[:, c2 * 512 + 256:c2 * 512 + 512], dftSc[c2 * 128:(c2 + 1) * 128, :])
            for f in range(2):
                p = ps()
                for c2 in range(2):
                    mm(p[:, 0:256], AB[:, NTL + c2, f * 128:(f + 1) * 128], wc_v[:, c2 * 512:c2 * 512 + 256],
                       start=(c2 == 0), stop=False)
                    mm(p[:, 0:256], AB[:, NTL + c2, 256 + f * 128:256 + (f + 1) * 128],
                       wc_v[:, c2 * 512 + 256:c2 * 512 + 512], start=False, stop=(c2 == 1))
                cp("act", ybr[:, 4 + f, SEQ:NTOK], p[:, 0:256])
            if "yd" in dbg and lreal == nl - 1:
                kb.dma("sp", dbg_out["d_yd"].rearrange("p (a b) -> p a b", a=2), ybr[:, 4:6, :])
            if stop_after == "yd":
                return

            def rope_proj(ci, dst, tbs, wa=None, ws=None, dst_off=None):
                if wa is None:
                    wa = load_win(ci)
                    ws = load_win(ci + 1)
                for fc in range(4):
                    for (t0, n) in tbs:
                        d0 = t0 if dst_off is None else dst_off
                        pa = ps()
                        fm_group(pa[:, 0:n], wa, fc * 128, t0, n)
                        pb = ps()
                        fm_group(pb[:, 0:n], ws, fc * 128, t0, n)
                        ct = next_tmp()
                        st_ = next_tmp()
                        kb.dma("sp", ct[:, 0:n], ropecos[:, t0:t0 + n])
                        kb.dma("sp", st_[:, 0:n], ropesin[:, t0:t0 + n])
                        tt("dve", ct[:, 0:n], pa[:, 0:n], ct[:, 0:n], ALU.mult)
                        tt("dve", st_[:, 0:n], pb[:, 0:n], st_[:, 0:n], ALU.mult)
                        tt("pool", dst[:, fc, d0:d0 + n], ct[:, 0:n], st_[:, 0:n], ALU.add)

            rope_proj(5, kT, TB)
            wv = load_win(7)
            memset("pool", vaug[:, :, :, 128:129], 1.0)
            for t in range(NT):
                p = ps()
                tm_group(p[:], wv, 0, 512, t)
                if t % 2 == 0:
                    cp("act", vaug[:, t, :, 0:128], p[:].rearrange("p (h e) -> p h e", h=4))
                else:
                    cp("dve", vaug[:, t, :, 0:128], p[:].rearrange("p (h e) -> p h e", h=4))
            if "qk" in dbg and lreal == nl - 1:
                kb.dma("sp", dbg_out["d_k"].rearrange("p (a b) -> p a b", a=4), kT[:])

            wq_a = load_win(3)
            wq_s = load_win(4)

            def attention(q0, nq, ktiles):
                nqs = nq // 128
                rope_proj(3, qblk, [(q0, nq)], wa=wq_a, ws=wq_s, dst_off=0)
                if "qk" in dbg and lreal == nl - 1:
                    for fc_ in range(4):
                        kb.dma("sp", dbg_out["d_q"][:, fc_ * NTOK + q0:fc_ * NTOK + q0 + nq], qblk[:, fc_, 0:nq])
                for h in range(4):
                    nacc = 2 * nqs
                    nb_ = (nacc + 2) // 3
                    accb = [ps(hold=True) for _ in range(nb_)]
                    for b_ in accb:
                        memset("dve", b_[:], 0.0)

                    def acc_ap(i, qs):
                        a = i * nqs + qs
                        return accb[a // 3][:, (a % 3) * 160:(a % 3) * 160 + 129]

                    steps = [(i, kt) for i in range(2) for kt in ktiles]
                    pend = []
                    nxt = 0
                    LOOK = 3
                    for s_ in range(len(steps)):
                        while nxt < len(steps) and len(pend) <= LOOK:
                            i_, kt_ = steps[nxt]
                            blk = h * 2 + i_
                            ch, pb_ = blk // 2, (blk % 2) * 64
                            sp_ = ps()
                            mm(sp_[:, 0:nq], kT[pb_:pb_ + 64, ch, kt_ * 128:(kt_ + 1) * 128],
                               qblk[pb_:pb_ + 64, ch, 0:nq], start=True, stop=True)
                            e_ = next_e()
                            act(e_[:, 0:nq], sp_[:, 0:nq], AF.Exp, scale=0.125)
                            pend.append(e_)
                            nxt += 1
                        e_ = pend.pop(0)
                        i, kt = steps[s_]
                        for qs in range(nqs):
                            mm(acc_ap(i, qs), e_[:, qs * 128:(qs + 1) * 128], vaug[:, kt, h, :],
                               start=False, stop=False, skip_group_check=True)
                    QS = list(range(nqs))
                    a1s = [acc_ap(0, qs) for qs in QS]
                    a2s = [acc_ap(1, qs) for qs in QS]
                    sts = [next_stat(8) for qs in QS]
                    os_ = [next_tmp() for qs in QS]
                    for qs in QS:
                        cp("dve", sts[qs][:, 0:1], a1s[qs][:, 128:129])
                        cp("dve", sts[qs][:, 1:2], a2s[qs][:, 128:129])
                    for qs in QS:
                        st = sts[qs]
                        kb.op("dve", lambda e: e.reciprocal(st[:, 2:4], st[:, 0:2]), reads=[st[:, 0:2]],
                              writes=[st[:, 2:4]])
                    for qs in QS:
                        st = sts[qs]
                        tt("dve", st[:, 3:4], st[:, 3:4], lam_s[:, 3:4], ALU.mult)
                    for qs in QS:
                        ts("dve", os_[qs][:, 0:128], a1s[qs][:, 0:128], sts[qs][:, 2:3], None, ALU.mult)
                    for qs in QS:
                        stt("dve", os_[qs][:, 0:128], a2s[qs][:, 0:128], sts[qs][:, 3:4], os_[qs][:, 0:128],
                            ALU.mult, ALU.add)
                    for qs in QS:
                        act(os_[qs][:, 128:256], os_[qs][:, 0:128], AF.Square)
                    for qs in QS:
                        o = os_[qs]
                        st = sts[qs]
                        kb.op("dve", lambda e: e.reduce_sum(st[:, 4:5], o[:, 128:256], AX.X),
                              reads=[o[:, 128:256]], writes=[st[:, 4:5]])
                    for qs in QS:
                        act(sts[qs][:, 5:6], sts[qs][:, 4:5], AF.Ln, bias=eps_t[:, 0:1], scale=1.0 / 128)
                    for qs in QS:
                        act(sts[qs][:, 5:6], sts[qs][:, 5:6], AF.Exp, scale=-0.5)
                    for qs in QS:
                        stt("dve", os_[qs][:, 256:384], os_[qs][:, 0:128], sts[qs][:, 5:6], subg[:], ALU.mult, ALU.mult)
                    pts = []
                    for qs in QS:
                        pt = ps()
                        tr(pt[:, 0:128], os_[qs][:, 256:384])
                        pts.append(pt)
                    for qs in QS:
                        cp("act", ybr[:, 6 + h, q0 + qs * 128:q0 + (qs + 1) * 128], pts[qs][:, 0:128])
                    for b_ in accb:
                        release(b_)

            for qb in range(4):
                attention(qb * 512, 512, list(range(NT)))
            if not last:
                attention(SEQ, CTX, [NTL, NTL + 1])
            if "yc" in dbg and lreal == nl - 1:
                kb.dma("sp", dbg_out["d_yc"].rearrange("p (a b) -> p a b", a=4), ybr[:, 6:10, :])
            if stop_after == "yc":
                return

            def load_p4(dc_):
                wg_ = next_wbuf()
                kb.dma("pool", wg_[:], w_gate[l, dc_].rearrange("(k p) n -> p k n", p=128))
                wp__ = wpb[dc_ % 2]
                kb.dma("pool", wp__[:], w_p[l, dc_].rearrange("(k p) n -> p k n", p=128))
                return wg_, wp__

            nxt_w = load_p4(0)
            for dc in range(8):
                wg, wp_ = nxt_w
                if dc + 1 < 8:
                    nxt_w = load_p4(dc + 1)
                chunks = [(0, 2), (2, 4), (6, 10), (4, 6)]
                for (t0, n) in tbl:
                    macc_t = next_xt()
                    for br in range(4):
                        pg = ps()
                        fm_group(pg[:, 0:n], wg, br * 128, t0, n)
                        pp = ps()
                        c0, c1 = chunks[br]
                        for c_ in range(c0, c1):
                            mm(pp[:, 0:n], wp_[:, c_, :], ybr[:, c_, t0:t0 + n], start=(c_ == c0), stop=(c_ == c1 - 1))
                        sg = next_tmp()
                        act(sg[:, 0:n], pg[:, 0:n], AF.Sigmoid, bias=bgT[:, br * 8 + dc:br * 8 + dc + 1])
                        if br == 0:
                            tt("dve", macc_t[:, 0:n], sg[:, 0:n], pp[:, 0:n], ALU.mult)
                        else:
                            tt("dve", sg[:, 0:n], sg[:, 0:n], pp[:, 0:n], ALU.mult)
                            if br < 3:
                                tt("pool", macc_t[:, 0:n], macc_t[:, 0:n], sg[:, 0:n], ALU.add)
                            else:
                                tt("pool", mT[:, dc, t0:t0 + n], macc_t[:, 0:n], sg[:, 0:n], ALU.add)
            if "mT" in dbg and lreal == nl - 1:
                kb.dma("sp", dbg_out["d_mT"].rearrange("p (a b) -> p a b", a=8), mT[:])

            kb.dma("sp", wr_sb[:], w_router[l].rearrange("(k p) n -> p k n", p=128))
            for kc in range(8):
                for s in range(2):
                    ts("pool", wr_mod[:, kc, s, :], wr_sb[:, kc, :], modT[:, 3, kc, s:s + 1], None, ALU.mult)
            for s in range(2):
                p = ps()
                for kc in range(8):
                    mm(p[:, 0:NE], modT[:, 2, kc, s:s + 1].to_broadcast([128, 128]), wr_sb[:, kc, :], start=(kc == 0), stop=(kc == 7))
                cp("dve", rb_bc[:, s, :], p[:, 0:NE])

            slot_order = [0, 4, 5, 1, 2, 3]

            def load_w(ee, which):
                w_ = ewslot[slot_order[(ee % 2) * 3 + which]]
                wsrc = (w_eg, w_eu, w_ed)[which]
                kb.dma("pool", w_[:], wsrc[l, ee].rearrange("(k p) n -> p k n", p=128))
                return w_

            W = {}
            for wh in range(3):
                W[(0, wh)] = load_w(0, wh)
            load_gate(0)
            wo_a = next_wbuf()
            wo_b = next_wbuf()
            kb.dma("pool", wo_a[:], w_o[l, :, 0:512].rearrange("(k p) n -> p k n", p=128))
            kb.dma("pool", wo_b[:], w_o[l, :, 512:1024].rearrange("(k p) n -> p k n", p=128))
            bcast_row(lngb[:, 0, :], ln_gb[l, 0:1, :])
            bcast_row(lngb[:, 1, :], ln_gb[l, 1:2, :])

            def residual_ln(x_old, sub_aps, gi, s, out_t):
                for half in range(2):
                    hs = slice(half * 512, (half + 1) * 512)
                    tt("dve", out_t[:, hs], sub_aps[half], gate_bc[:, s, hs], ALU.mult)
                stt("pool", out_t[:], x_old[:], DN_ALPHA, out_t[:], ALU.mult, ALU.add)
                mean, rstd = ln_stats(out_t, D)
                act(out_t[:], out_t[:], AF.Identity, bias=ln_stats.nbias, scale=rstd)
                tt("dve", out_t[:], out_t[:], lngb[:, 0, :], ALU.mult)
                tt("dve", out_t[:], out_t[:], lngb[:, 1, :], ALU.add)

            def wo_stage_a(t):
                s = 0 if t < NTL else 1
                pa = ps()
                pb = ps()
                for kc in range(8):
                    mm(pa[:], mT[:, kc, t * 128:(t + 1) * 128], wo_a[:, kc, :], start=(kc == 0), stop=(kc == 7))
                for kc in range(8):
                    mm(pb[:], mT[:, kc, t * 128:(t + 1) * 128], wo_b[:, kc, :], start=(kc == 0), stop=(kc == 7))
                x_old = next_xt()
                kb.dma("sp", x_old[:], xbuf[t * 128:(t + 1) * 128, :])
                x_new = next_xt()
                residual_ln(x_old, (pa[:], pb[:]), 0, s, x_new)
                kb.dma("pool", xbuf[t * 128:(t + 1) * 128, :], x_new[:])
                return x_new

            def wo_stage_b(t, x_new):
                s = 0 if t < NTL else 1
                normalize_to_hT(x_new, t, 2, dst=None, also_xn2=True, router=True)
                pl = ps()
                for kc in range(8):
                    mm(pl[:, 0:NE], xnT_r[:, kc, :], wr_mod[:, kc, s, :], start=(kc == 0), stop=(kc == 7))
                lg = next_stat(16)
                tt("dve", lg, pl[:, 0:NE], rb_bc[:, s, :], ALU.add)
                st = next_stat(4)
                kb.op("dve", lambda e: e.reduce_max(st[:, 0:1], lg, AX.X), reads=[lg], writes=[st[:, 0:1]])
                ts("dve", st[:, 1:2], st[:, 0:1], -1.0, None, ALU.mult)
                act(lg, lg, AF.Exp, bias=st[:, 1:2])
                kb.op("dve", lambda e: e.reduce_sum(st[:, 2:3], lg, AX.X), reads=[lg], writes=[st[:, 2:3]])
                kb.op("dve", lambda e: e.reciprocal(st[:, 3:4], st[:, 2:3]), reads=[st[:, 2:3]], writes=[st[:, 3:4]])
                ts("dve", lg, lg, st[:, 3:4], None, ALU.mult)
                pt = ps()
                tr(pt[0:NE, 0:128], lg)
                cp("dve", affT[:, t * 128:(t + 1) * 128], pt[0:NE, 0:128])

            xnew_cur = wo_stage_a(0)
            for t in range(ntl):
                xnew_nxt = wo_stage_a(t + 1) if t + 1 < ntl else None
                wo_stage_b(t, xnew_cur)
                xnew_cur = xnew_nxt
            if "xmid" in dbg and lreal == nl - 1:
                for t in range(ntl):
                    x_t = next_xt()
                    kb.dma("sp", x_t[:], xbuf[t * 128:(t + 1) * 128, :])
                    kb.dma("sp", dbg_out["d_xmid"][t * 128:(t + 1) * 128, :], x_t[:])
                kb.dma("sp", dbg_out["d_aff"], affT[:])
            if stop_after == "mid":
                return

            def topk(src0, n, cap, dst0):
                cp("pool", affW[:, src0:src0 + n], affT[:, src0:src0 + n])
                for r in range(cap // 8):
                    mv = topv[:, dst0 + r * 8:dst0 + (r + 1) * 8]
                    kb.op("dve", lambda e: e.max(mv, affW[:, src0:src0 + n]), reads=[affW[:, src0:src0 + n]],
                          writes=[mv])
                    mi_ = topi[:, dst0 + r * 8:dst0 + (r + 1) * 8]
                    kb.op("dve", lambda e: e.max_index(mi_, mv, affW[:, src0:src0 + n]),
                          reads=[mv, affW[:, src0:src0 + n]], writes=[mi_])
                    if r < cap // 8 - 1:
                        kb.op("dve", lambda e: e.match_replace(affW[:, src0:src0 + n], mv, affW[:, src0:src0 + n], -1.0),
                              reads=[mv, affW[:, src0:src0 + n]], writes=[affW[:, src0:src0 + n]])

            topk(0, SEQ, CAPL, 0)
            if not last:
                topk(SEQ, CTX, CAPC, CAPL)
            ncap = CAPL if last else CAPL + CAPC
            cp("dve", topif[:, 0:ncap], topi[:, 0:ncap])
            if not last:
                ts("dve", topif[:, CAPL:ncap], topif[:, CAPL:ncap], float(SEQ), None, ALU.add)
            njt = 2 if last else 3
            for j in range(njt):
                w_ = 128 if j < 2 else CAPC
                p = ps()
                tr(p[0:w_, 0:NE], topif[:, j * 128:j * 128 + w_])
                cp("dve", idxT[0:w_, j, :], p[0:w_, 0:NE])
                p2 = ps()
                tr(p2[0:w_, 0:NE], topv[:, j * 128:j * 128 + w_])
                cp("dve", gatT[0:w_, j, :], p2[0:w_, 0:NE])
            if "topk" in dbg and lreal == nl - 1:
                kb.dma("sp", dbg_out["d_topv"], topv[:])
                kb.dma("sp", dbg_out["d_topi"], topif[:])
            for t in range(ntl):
                kb.dma("sp", macc[t * 128:(t + 1) * 128, :], zeros_d)

            def gather_rows(dst, idx_col):
                kb.dma_raw("pool", lambda e: e.indirect_dma_start(
                    out=dst, out_offset=None, in_=xn2[:, :],
                    in_offset=bass.IndirectOffsetOnAxis(ap=idx_col, axis=0)),
                    reads=[xn2[:, :], idx_col], writes=[dst])

            def transpose_mod(src_t, dst3, col0, ncols, s):
                for half in range(2):
                    p = ps()
                    for k4 in range(4):
                        kc = half * 4 + k4
                        tr(p[:, k4 * 128:k4 * 128 + ncols], src_t[0:ncols, kc * 128:(kc + 1) * 128])
                    for k4 in range(4):
                        kc = half * 4 + k4
                        if half == 0:
                            act(dst3[:, kc, col0:col0 + ncols], p[:, k4 * 128:k4 * 128 + ncols], AF.Identity,
                                bias=modT[:, 2, kc, s:s + 1], scale=modT[:, 3, kc, s:s + 1])
                        else:
                            ts("dve", dst3[:, kc, col0:col0 + ncols], p[:, k4 * 128:k4 * 128 + ncols],
                               modT[:, 3, kc, s:s + 1], modT[:, 2, kc, s:s + 1], ALU.mult, ALU.add)

            gtiles = [wbuf[i][:].rearrange("p k n -> p (k n)").bitcast(F32)[:, h_ * D:(h_ + 1) * D]
                      for i in range(NW) for h_ in range(2)]
            gt_i = [0]

            def next_gt():
                g_ = gtiles[gt_i[0] % len(gtiles)]
                gt_i[0] += 1
                return g_

            xeT2 = arena[:, 6656:6656 + 2304].rearrange("p (k n) -> p k n", k=8)
            xeTs = [xeT, xeT2]
            nsl = CAPL if last else CAPL + CAPC

            def do_gather(ee):
                gts = []
                for j in range(njt):
                    g_ = next_gt()
                    gather_rows(g_, idxT[:, j, ee:ee + 1])
                    gts.append(g_)
                return gts

            def do_transposes(ee, gts):
                for j in range(njt):
                    w_ = 128 if j < 2 else CAPC
                    transpose_mod(gts[j], xeTs[ee % 2], j * 128, w_, 0 if j < 2 else 1)

            for wh in range(3):
                W[(1, wh)] = load_w(1, wh)
            G = {0: do_gather(0)}
            if NE > 1:
                G[1] = do_gather(1)
            do_transposes(0, G[0])
            prev_sc = []
            for e_ in range(NE):
                xe_ = xeTs[e_ % 2]
                wk_g, wk_u, wk_d = W[(e_, 0)], W[(e_, 1)], W[(e_, 2)]
                if e_ + 2 < NE:
                    G[e_ + 2] = do_gather(e_ + 2)
                for fc in range(8):
                    pg = ps()
                    pu = ps()
                    for kc in range(8):
                        mm(pg[:, 0:nsl], wk_g[:, kc, fc * 128:(fc + 1) * 128], xe_[:, kc, 0:nsl],
                           start=(kc == 0), stop=(kc == 7))
                    for kc in range(8):
                        mm(pu[:, 0:nsl], wk_u[:, kc, fc * 128:(fc + 1) * 128], xe_[:, kc, 0:nsl],
                           start=(kc == 0), stop=(kc == 7))
                    sg = next_tmp()
                    act(sg[:, 0:nsl], pg[:, 0:nsl], AF.Silu)
                    tt("dve", hidT[:, fc, 0:nsl], sg[:, 0:nsl], pu[:, 0:nsl], ALU.mult)
                if e_ + 2 < NE:
                    W[(e_ + 2, 0)] = load_w(e_ + 2, 0)
                    W[(e_ + 2, 1)] = load_w(e_ + 2, 1)
                if e_ + 1 < NE:
                    do_transposes(e_ + 1, G[e_ + 1])
                yes = []
                for j in range(njt):
                    w_ = 128 if j < 2 else CAPC
                    ye = next_xt()
                    for half in range(2):
                        p = ps()
                        for fc in range(8):
                            mm(p[0:w_, :], hidT[:, fc, j * 128:j * 128 + w_], wk_d[:, fc, half * 512:(half + 1) * 512],
                               start=(fc == 0), stop=(fc == 7))
                        if half == 0:
                            act(ye[0:w_, 0:512], p[0:w_, :], AF.Copy, scale=gatT[0:w_, j, e_:e_ + 1])
                        else:
                            ts("dve", ye[0:w_, 512:1024], p[0:w_, :], gatT[0:w_, j, e_:e_ + 1], None, ALU.mult)
                    yes.append(ye)
                cur_sc = []
                for j in range(njt):
                    ye = yes[j]
                    ev_ = kb.dma_raw("pool", lambda e: e.indirect_dma_start(
                        out=macc[:, :], out_offset=bass.IndirectOffsetOnAxis(ap=idxT[:, j, e_:e_ + 1], axis=0),
                        in_=ye[:], in_offset=None, compute_op=ALU.add),
                        reads=[ye[:], idxT[:, j, e_:e_ + 1]] + ([macc[:, :]] if e_ == 0 else []),
                        writes=[], extra_deps=(prev_sc + cur_sc) if e_ == NE - 1 else prev_sc,
                        record_writes=([macc[:, :]] if (e_ == NE - 1 and j == njt - 1) else []))
                    cur_sc.append(ev_)
                prev_sc = cur_sc
                if e_ + 2 < NE:
                    W[(e_ + 2, 2)] = load_w(e_ + 2, 2)
            if "macc" in dbg and lreal == nl - 1:
                for t in range(ntl):
                    x_t = next_xt()
                    kb.dma("sp", x_t[:], macc[t * 128:(t + 1) * 128, :])
                    kb.dma("sp", dbg_out["d_macc"][t * 128:(t + 1) * 128, :], x_t[:])

            load_gate(1)
            li_ = layers.index(lreal)
            nxt_l = layers[li_ + 1] if (not last and li_ + 1 < len(layers) and stop_after is None) else None
            if nxt_l is not None:
                layer(nxt_l, part="p0")
            bcast_row(lngb[:, 0, :], ln_gb[l, 2:3, :])
            bcast_row(lngb[:, 1, :], ln_gb[l, 3:4, :])
            for t in range(ntl):
                s = 0 if t < NTL else 1
                x_old = next_xt()
                kb.dma("sp", x_old[:], xbuf[t * 128:(t + 1) * 128, :])
                x_new = next_xt()
                m_t = next_xt()
                kb.dma("sp", m_t[:], macc[t * 128:(t + 1) * 128, :])
                residual_ln(x_old, (m_t[:, 0:512], m_t[:, 512:1024]), 1, s, x_new)
                if last:
                    kb.dma("pool", y[t * 128:(t + 1) * 128, :], x_new[:])
                else:
                    kb.dma("pool", xbuf[t * 128:(t + 1) * 128, :], x_new[:])
                if nxt_l is not None:
                    xn_f = norm_stage(x_new, t)
                    tr_stage(xn_f, t, 0, dst=hT)
            if nxt_l is not None:
                p1_done.add(nxt_l)

        for nm, shp, dt_ in (("d_hT", [128, 8 * NTOK], BF16), ("d_ya", [128, 2 * NTOK], BF16),
                             ("d_yb", [128, 2 * NTOK], BF16), ("d_yd", [128, 2 * NTOK], BF16),
                             ("d_q", [128, 4 * NTOK], BF16), ("d_k", [128, 4 * NTOK], BF16),
                             ("d_yc", [128, 4 * NTOK], BF16), ("d_mT", [128, 8 * NTOK], BF16),
                             ("d_xmid", [NTOK, D], F32), ("d_aff", [NE, NTOK], F32),
                             ("d_topv", [NE, CAPL + CAPC], F32), ("d_topi", [NE, CAPL + CAPC], F32),
                             ("d_macc", [NTOK, D], F32), ("d_xend", [NTOK, D], F32)):
            if dbg:
                dout(nm, shp, dt_)
        for l in layers:
            if stop_after != "C":
                layer(l)
        if dbg:
            for t in range(NT):
                x_t = next_xt()
                kb.dma("sp", x_t[:], xbuf[t * 128:(t + 1) * 128, :])
                kb.dma("sp", dbg_out["d_xend"][t * 128:(t + 1) * 128, :], x_t[:])
        kb.finish()
        stats = (kb.n_inst, kb.n_wait)
    return nc, dbg_out, stats


def prep_inputs(inputs, layers=None):
    if layers is None:
        layers = list(range(DEPTH))
    f = lambda a: np.ascontiguousarray(np.asarray(a, dtype=np.float32)[layers])
    perm = _w_in_perm()
    shared = {
        "w_ada": f(inputs["w_ada"]), "b_ada": f(inputs["b_ada"]),
        "w_in": f(np.asarray(inputs["w_in"])[:, :, perm]),
        "w_gate": f(np.asarray(inputs["w_gate"]).reshape(DEPTH, D, 4, 8, 128).transpose(0, 3, 1, 2, 4).reshape(DEPTH, 8, D, 512)),
        "b_gate": f(inputs["b_gate"]),
        "sgu_ln_g": f(inputs["sgu_ln_g"]), "sgu_ln_b": f(inputs["sgu_ln_b"]),
        "w_spT": f(np.transpose(np.asarray(inputs["w_sp"]), (0, 1, 3, 2))),
        "b_sp": f(inputs["b_sp"]), "conv_w": f(inputs["conv_w"]),
        "lam_in": f(np.stack([np.asarray(inputs[k]) for k in ("lam_q1", "lam_k1", "lam_q2", "lam_k2")], axis=1)),
        "subln_g": f(inputs["subln_g"]),
        "w_p": f(np.concatenate([np.asarray(inputs[k]) for k in ("w_pa", "w_pb", "w_pd", "w_pc")], axis=1)
                 .reshape(DEPTH, 1280, 8, 128).transpose(0, 2, 1, 3)),
        "w_o": f(inputs["w_o"]),
        "ln_gb": f(np.stack([np.asarray(inputs[k]) for k in ("ln1_g", "ln1_b", "ln2_g", "ln2_b")], axis=1)),
        "w_router": f(inputs["w_router"]),
        "w_exp_gate": f(inputs["w_exp_gate"]), "w_exp_up": f(inputs["w_exp_up"]),
        "w_exp_down": f(inputs["w_exp_down"]),
    }
    shared.update(_get_consts())
    x = np.asarray(inputs["x"], dtype=np.float32)
    ctx = np.asarray(inputs["ctx"], dtype=np.float32)
    c = np.asarray(inputs["c"], dtype=np.float32)
    c_ctx = np.asarray(inputs["c_ctx"], dtype=np.float32)
    in_maps = []
    for b in range(x.shape[0]):
        m = dict(shared)
        m["xin"] = np.ascontiguousarray(np.concatenate([x[b], ctx[b]], axis=0))
        m["cc"] = np.ascontiguousarray(np.stack([c[b], c_ctx], axis=0))
        in_maps.append(m)
    return in_maps


_PROG = {}


def kernel(**inputs):
    in_maps = prep_inputs(inputs)
    if "nc" not in _PROG:
        _PROG["nc"] = build()[0]
    nc = _PROG["nc"]
    res = run_bass_kernel_spmd(nc, in_maps, core_ids=list(range(len(in_maps))))
    out = np.stack([np.asarray(r["y"], dtype=np.float32) for r in res.results], axis=0)
    return out
```
